# Optimizing a Trainium2 kernel written in Bass

```python
import math
import jax, jax.numpy as jnp
from jax import lax
import numpy as np

D_MODEL = 1024
BATCH = 4
SEQ = 8192
DEPTH = 1

ATTN_HEADS = 8
ATTN_HEAD_DIM = 64
ATTN_WIDTH = ATTN_HEADS * ATTN_HEAD_DIM
DILATED_PATTERNS = ((128, 1), (512, 4), (2048, 16))
ATTN_BLOCK = 128
N_BUCKETS = 32
MAX_DISTANCE = 2048
NEG_INF = -1e30
HGRN_HEADS = 8
HGRN_KEY_DIM = 128
HGRN_VAL_DIM = 128
HGRN_FDIM = HGRN_HEADS * HGRN_KEY_DIM
HGRN_WIDTH = HGRN_HEADS * HGRN_VAL_DIM
HGRN_CHUNK = 64
IN_WIDTH = 4 * ATTN_WIDTH + 2 * HGRN_FDIM + 2 * HGRN_WIDTH + 2 * D_MODEL
EPS = 1e-6

kernel_name = "hybrid_dilated_attn_hgrn2_gated_merge"


def rmsnorm(x, g):
    xf = x.astype(jnp.float32)
    y = xf * lax.rsqrt(jnp.mean(xf * xf, axis=-1, keepdims=True) + EPS)
    return (y * g.astype(jnp.float32)).astype(x.dtype)


def t5_bucket(dist):
    max_exact = N_BUCKETS // 2
    n = dist.astype(jnp.float32)
    large = max_exact + (jnp.log(jnp.maximum(n, 1.0) / max_exact)
                         / math.log(MAX_DISTANCE / max_exact)
                         * (N_BUCKETS - max_exact)).astype(jnp.int32)
    large = jnp.minimum(large, N_BUCKETS - 1)
    return jnp.where(dist < max_exact, dist, large)


def dilated_pattern(q, k, v, rel_bias, window, dilation):
    B, S, H, E = q.shape
    L = S // dilation
    span = window // dilation
    nb = -(-L // ATTN_BLOCK)
    Lp = nb * ATTN_BLOCK

    def to_sub(t):
        t = t.reshape(B, L, dilation, H, E).transpose(0, 2, 3, 1, 4)
        return jnp.pad(t, ((0, 0), (0, 0), (0, 0), (0, Lp - L), (0, 0)))

    def kv_blocks(t):
        tp = jnp.pad(to_sub(t), ((0, 0), (0, 0), (0, 0), (ATTN_BLOCK, 0), (0, 0)))
        prev = tp[:, :, :, :Lp].reshape(B, dilation, H, nb, ATTN_BLOCK, E)
        cur = tp[:, :, :, ATTN_BLOCK:].reshape(B, dilation, H, nb, ATTN_BLOCK, E)
        return jnp.concatenate([prev, cur], axis=4)

    qs = to_sub(q).reshape(B, dilation, H, nb, ATTN_BLOCK, E)
    ks, vs = kv_blocks(k), kv_blocks(v)

    qi = jnp.arange(ATTN_BLOCK)[:, None]
    kj = jnp.arange(2 * ATTN_BLOCK)[None, :]
    delta = qi + ATTN_BLOCK - kj
    band = (delta >= 0) & (delta <= span)
    key_pos = jnp.arange(nb)[:, None, None] * ATTN_BLOCK + kj[None] - ATTN_BLOCK
    mask = band[None] & (key_pos >= 0)
    bucket = t5_bucket(jnp.clip(delta, 0, None) * dilation)
    bias = rel_bias.astype(jnp.float32)[bucket].transpose(2, 0, 1)

    s = jnp.einsum('bdhnqe,bdhnke->bdhnqk', qs, ks) * (E ** -0.5) + bias[None, None, :, None]
    s = jnp.where(mask, s, NEG_INF)
    m = jnp.max(s, axis=-1, keepdims=True)
    p = jnp.exp(s - m)
    den = jnp.sum(p, axis=-1, keepdims=True)
    o = jnp.einsum('bdhnqk,bdhnke->bdhnqe', p, vs) / den
    lse = (m + jnp.log(den))[..., 0]

    o = o.reshape(B, dilation, H, Lp, E)[:, :, :, :L].transpose(0, 3, 1, 2, 4).reshape(B, S, H, E)
    lse = lse.reshape(B, dilation, H, Lp)[..., :L].transpose(0, 3, 1, 2).reshape(B, S, H)
    return o, lse


def dilated_attention(q, k, v, rel_bias):
    outs, lses = [], []
    for window, dilation in DILATED_PATTERNS:
        o, lse = dilated_pattern(q, k, v, rel_bias, window, dilation)
        outs.append(o)
        lses.append(lse)
    w = jax.nn.softmax(jnp.stack(lses, 0), axis=0)
    return jnp.einsum('gbsh,gbshe->bshe', w, jnp.stack(outs, 0))


def hgrn2_recurrence(q, f_raw, i, lb):
    B, S, H, DK = q.shape
    DV = i.shape[-1]
    C = HGRN_CHUNK
    nc = S // C
    f = lb + (1.0 - lb) * jax.nn.sigmoid(f_raw)
    g = jnp.log(f)
    k = 1.0 - f

    def chunks(t):
        return t.reshape(B, nc, C, H, t.shape[-1]).transpose(1, 0, 3, 2, 4)

    causal = jnp.tril(jnp.ones((C, C), dtype=bool))

    def step(state, inp):
        qc, kc, vc, gc = inp
        b = jnp.cumsum(gc, axis=2)
        o_inter = jnp.einsum('bhtk,bhkv->bhtv', qc * jnp.exp(b), state)
        diff = b[:, :, :, None, :] - b[:, :, None, :, :]
        decay = jnp.exp(jnp.where(causal[:, :, None], diff, -jnp.inf))
        a = jnp.einsum('bhtk,bhsk,bhtsk->bhts', qc, kc, decay)
        o_intra = jnp.einsum('bhts,bhsv->bhtv', a, vc)
        b_last = b[:, :, -1:, :]
        new_state = (jnp.exp(b_last[:, :, 0, :])[..., None] * state
                     + jnp.einsum('bhsk,bhsv->bhkv', kc * jnp.exp(b_last - b), vc))
        return new_state, o_inter + o_intra

    s0 = jnp.zeros((B, H, DK, DV), jnp.float32)
    _, o = lax.scan(step, s0, (chunks(q), chunks(k), chunks(i), chunks(g)))
    return o.transpose(1, 0, 3, 2, 4).reshape(B, S, H, DV)


def setup_inputs(seed: int = 0) -> dict:
    key = jax.random.key(seed)
    ks = jax.random.split(key, 13)
    D = D_MODEL
    nrm = lambda k, shape, fan_in: jax.random.normal(k, shape, jnp.float32) * fan_in ** -0.5
    return {
        "x": jax.random.normal(ks[0], (BATCH, SEQ, D), jnp.float32),
        "c": jax.random.normal(ks[1], (BATCH, D), jnp.float32),
        "w_ada": nrm(ks[2], (DEPTH, D, 3 * D), D),
        "b_ada": 0.02 * jax.random.normal(ks[3], (DEPTH, 3 * D), jnp.float32),
        "norm_g": 1.0 + 0.05 * jax.random.normal(ks[4], (DEPTH, D), jnp.float32),
        "w_in": nrm(ks[5], (DEPTH, D, IN_WIDTH), D),
        "hgrn_onorm_g": 1.0 + 0.05 * jax.random.normal(ks[6], (DEPTH, HGRN_VAL_DIM), jnp.float32),
        "w_branch_a": nrm(ks[7], (DEPTH, ATTN_WIDTH, D), ATTN_WIDTH),
        "w_branch_b": nrm(ks[8], (DEPTH, HGRN_WIDTH, D), HGRN_WIDTH),
        "w_out": nrm(ks[9], (DEPTH, D, D), D),
        "rel_bias": 0.5 * jax.random.normal(ks[10], (N_BUCKETS, ATTN_HEADS), jnp.float32),
        "hgrn_lb": 0.5 * jax.random.normal(ks[11], (DEPTH + 1, HGRN_FDIM), jnp.float32),
        "final_g": 1.0 + 0.05 * jax.random.normal(ks[12], (D,), jnp.float32),
    }


def reference(x, c, w_ada, b_ada, norm_g, w_in, hgrn_onorm_g, w_branch_a, w_branch_b,
              w_out, rel_bias, hgrn_lb, final_g):
    B, S, D = x.shape
    sizes = [ATTN_WIDTH] * 4 + [HGRN_FDIM, HGRN_FDIM, HGRN_WIDTH, HGRN_WIDTH, D_MODEL, D_MODEL]
    cuts = [int(v) for v in np.cumsum(sizes)[:-1]]
    lower_bounds = jnp.cumsum(jax.nn.softmax(hgrn_lb.astype(jnp.float32), axis=0), axis=0)
    for l in range(DEPTH):
        mod = jax.nn.silu(c) @ w_ada[l] + b_ada[l]
        shift, scale, gate = jnp.split(mod, 3, axis=-1)
        h = rmsnorm(x, norm_g[l]) * (1.0 + scale[:, None]) + shift[:, None]
        proj = h @ w_in[l]
        q_a, k_a, v_a, z_a, q_b, f_b, i_b, z_b, g_a, g_b = jnp.split(proj, cuts, axis=-1)

        heads_a = lambda t: t.astype(jnp.float32).reshape(B, S, ATTN_HEADS, ATTN_HEAD_DIM)
        o_a = dilated_attention(heads_a(q_a), heads_a(k_a), heads_a(v_a), rel_bias)
        o_a = o_a.reshape(B, S, ATTN_WIDTH).astype(x.dtype) * jax.nn.silu(z_a)

        heads_b = lambda t, e: t.astype(jnp.float32).reshape(B, S, HGRN_HEADS, e)
        lb = lower_bounds[l].reshape(HGRN_HEADS, HGRN_KEY_DIM)
        o_b = hgrn2_recurrence(jax.nn.silu(heads_b(q_b, HGRN_KEY_DIM)), heads_b(f_b, HGRN_KEY_DIM),
                               heads_b(i_b, HGRN_VAL_DIM), lb)
        o_b = rmsnorm(o_b, hgrn_onorm_g[l]).reshape(B, S, HGRN_WIDTH).astype(x.dtype) * jax.nn.silu(z_b)

        y = jax.nn.sigmoid(g_a) * (o_a @ w_branch_a[l]) + jax.nn.sigmoid(g_b) * (o_b @ w_branch_b[l])
        x = x + gate[:, None] * (y @ w_out[l])
    return rmsnorm(x, final_g)
```

```python
import math
from contextlib import ExitStack

import numpy as np
import concourse.bass as bass
import concourse.mybir as mybir
from concourse.bass_utils import run_bass_kernel_spmd

F32 = mybir.dt.float32
BF16 = mybir.dt.bfloat16
ALU = mybir.AluOpType
AF = mybir.ActivationFunctionType

D = 1024
SEQ = 8192
NB = 4
TOWN = 4096
TT = 512
NT = TOWN // TT
NEG = -30000.0
EPS = 1e-6
DEBUG = False
SKIP = {}
MERGE = True
SIMLAT = 60.0
PESWITCH = 2000.0


class Buf:
    __slots__ = ("name", "lastw", "readers", "dsem", "dcnt")

    def __init__(self, name):
        self.name = name
        self.lastw = None
        self.readers = []
        self.dsem = None
        self.dcnt = 0


class Instr:
    __slots__ = ("id", "eng", "fn", "deps", "is_dma", "dsem", "dval", "lidx", "marked", "seq", "waits")


class Sched:
    ENGS = ("pe", "act", "dve", "pool", "sp")

    def __init__(self, nc, stack):
        self.nc = nc
        self.stack = stack
        self.ins = []
        self.q = {e: [] for e in self.ENGS}
        self.esem = {e: stack.enter_context(nc.semaphore("s_" + e)) for e in self.ENGS if e != "sp"}
        self.capture = None

    def begin_capture(self):
        self.capture = []

    def end_capture(self):
        c = self.capture
        self.capture = None
        return c

    def merge(self, streams):
        eng_free = {e: 0.0 for e in self.ENGS}
        bw, br = {}, {}
        idx = [0] * len(streams)
        order = []
        last_pe = -1
        total = sum(len(st) for st in streams)
        while len(order) < total:
            best = None
            for si, st in enumerate(streams):
                if idx[si] >= len(st):
                    continue
                eng, fn, reads, writes, dma, cost, nbytes = st[idx[si]]
                t = eng_free[eng]
                for b in reads:
                    t = max(t, bw.get(b, 0.0))
                for b in writes:
                    t = max(t, bw.get(b, 0.0), br.get(b, 0.0))
                if eng == "pe" and last_pe >= 0 and last_pe != si:
                    t += PESWITCH
                key = (t, idx[si] / float(len(st)))
                if best is None or key < best[0]:
                    best = (key, si, t)
            _, si, t = best
            o = streams[si][idx[si]]
            idx[si] += 1
            eng, fn, reads, writes, dma, cost, nbytes = o
            c = cost if cost else 300.0
            if eng == "pe":
                last_pe = si
            if dma:
                eng_free[eng] = t + 60.0
                done = t + 2000.0 + nbytes / 200.0
            else:
                eng_free[eng] = t + c
                done = t + c + SIMLAT
            for b in reads:
                br[b] = max(br.get(b, 0.0), done)
            for b in writes:
                bw[b] = done
            order.append(o)
        for (eng, fn, reads, writes, dma, cost, nbytes) in order:
            self.op(eng, fn, reads, writes, dma)

    def op(self, eng, fn, reads=(), writes=(), dma=False, cost=None, nbytes=0):
        if self.capture is not None:
            self.capture.append((eng, fn, tuple(reads), tuple(writes), dma, cost, nbytes))
            return None
        i = Instr()
        i.id = len(self.ins)
        i.eng = eng
        i.fn = fn
        i.is_dma = dma
        i.marked = False
        i.seq = 0
        i.waits = []
        deps = set()
        for b in reads:
            if b.lastw is not None:
                deps.add(b.lastw)
        for b in writes:
            if b.lastw is not None:
                deps.add(b.lastw)
            for r in b.readers:
                deps.add(r)
        i.deps = deps
        if dma:
            tgt = writes[0] if writes else reads[0]
            if tgt.dsem is None:
                tgt.dsem = self.stack.enter_context(self.nc.semaphore("d_" + tgt.name))
            tgt.dcnt += 16
            i.dsem = tgt.dsem
            i.dval = tgt.dcnt
        for b in reads:
            b.readers.append(i.id)
        for b in writes:
            b.lastw = i.id
            b.readers = []
        i.lidx = len(self.q[eng])
        self.q[eng].append(i)
        self.ins.append(i)
        return i

    def finalize(self):
        for e in self.ENGS:
            waited = {}
            for i in self.q[e]:
                for d in sorted(i.deps):
                    di = self.ins[d]
                    if di.is_dma:
                        key = ("d", id(di.dsem))
                        if waited.get(key, 0) >= di.dval:
                            continue
                        waited[key] = di.dval
                        i.waits.append((di.dsem, di.dval, None))
                    else:
                        if di.eng == e and e == "pe":
                            continue
                        key = ("e", di.eng)
                        if waited.get(key, -1) >= di.lidx:
                            continue
                        waited[key] = di.lidx
                        di.marked = True
                        i.waits.append((None, None, di))
        for e in self.ENGS:
            c = 0
            for i in self.q[e]:
                if i.marked:
                    c += 1
                    i.seq = c

    def emit_engine(self, e, engine, final_waits=()):
        for i in self.q[e]:
            for (sem, val, di) in i.waits:
                if di is None:
                    engine.wait_ge(sem, val)
                else:
                    engine.wait_ge(self.esem[di.eng], di.seq)
            r = i.fn(engine)
            if i.is_dma:
                r.then_inc(i.dsem, 16)
            elif i.marked:
                r.then_inc(self.esem[e], 1)
        for (sem, val) in final_waits:
            engine.wait_ge(sem, val)


def build_program(n_own_tiles=NT, n_pre_tiles=NT, debug=False):
    nc = bass.Bass("TRN2", target_bir_lowering=False)
    dt_in = lambda name, shape, dt=F32: nc.dram_tensor(name, shape, dt, kind="ExternalInput").ap()
    xo = dt_in("xo", [TOWN, D])
    xp = dt_in("xp", [TOWN, D])
    cfm = dt_in("cfm", [128, 8])
    w_ada = dt_in("w_ada", [D, 3 * D])
    bada_fm = dt_in("bada_fm", [128, 16])
    bgate_row = dt_in("bgate_row", [1, D])
    ng_fm = dt_in("ng_fm", [128, 8])
    w_in = dt_in("w_in", [D, 8192])
    gn_col = dt_in("gn_col", [128, 1])
    w_a = dt_in("w_a", [512, D])
    w_b = dt_in("w_b", [D, D])
    w_o = dt_in("w_o", [D, D])
    bm_in = dt_in("bm", [128, 24 * 256])
    lb_fm = dt_in("lb_fm", [128, 16])
    fg_row = dt_in("fg_row", [1, D])
    flag_in = dt_in("flag", [128, 1])
    out = nc.dram_tensor("out", [TOWN, D], F32, kind="ExternalOutput").ap()
    win_bf = nc.dram_tensor("win_bf", [16, D, 512], BF16, kind="Internal").ap()
    wa_bf = nc.dram_tensor("wa_bf", [2, 512, 512], BF16, kind="Internal").ap()
    wb_bf = nc.dram_tensor("wb_bf", [2, D, 512], BF16, kind="Internal").ap()
    wo_bf = nc.dram_tensor("wo_bf", [2, D, 512], BF16, kind="Internal").ap()
    dbg = {}
    if debug:
        for nm, shp, dt in [("dbg_hT", [128, 8 * 512], BF16), ("dbg_q", [128, 4 * 512], BF16),
                            ("dbg_oa", [128, 4 * 512], BF16), ("dbg_ob", [128, 8 * 512], BF16),
                            ("dbg_y", [128, 8 * 512], BF16), ("dbg_mod", [128, 16], F32),
                            ("dbg_S", [128, 8 * 128], F32)]:
            dbg[nm] = nc.dram_tensor(nm, shp, dt, kind="ExternalOutput").ap()

    with ExitStack() as st:
        S = Sched(nc, st)
        sb = lambda name, shape, dt: st.enter_context(nc.sbuf_tensor(name, shape, dt))
        ps = lambda name, shape, dt: st.enter_context(nc.psum_tensor(name, shape, dt))
        bufs = {}

        def B(name):
            if name not in bufs:
                bufs[name] = Buf(name)
            return bufs[name]

        BM = sb("BM", [128, 24, 256], BF16)
        ident_f = sb("ident_f", [128, 128], F32)
        ident_b = sb("ident_b", [128, 128], BF16)
        ones_b = sb("ones_b", [128, 64], BF16)
        ones_f = sb("ones_f", [128, 128], F32)
        cmask = sb("cmask", [128, 64], F32)
        gateB = sb("gateB", [128, D], F32)
        fgB = sb("fgB", [128, D], F32)
        small = sb("small", [128, 128], F32)
        GS, SH, LBV, OML, GN, FLAG, NBIAS, EPSC, SS, RSTD, CS = 0, 8, 16, 24, 32, 33, 34, 38, 39, 43, 47
        CS2 = 56
        TMPC = 72
        Sst = sb("Sst", [128, 8, 128], F32)
        NWB = 3
        wbuf = [sb("wbuf%d" % i, [128, 8, 512], BF16) for i in range(NWB)]
        XB = [sb("XB%d" % i, [128, D], F32) for i in range(2)]
        SC0t = sb("SC0t", [128, 1024], F32)
        SC1t = sb("SC1t", [128, 1024], F32)
        XN = SC0t
        JUNK = SC0t[:].bitcast(BF16)[:, 0:D]
        PT0t = sb("PT0t", [128, 1024], BF16)
        PT1t = sb("PT1t", [128, 1024], BF16)
        VA0t = sb("VA0t", [128, 2048], BF16)
        VA1t = sb("VA1t", [128, 1024], BF16)
        szb = sb("szb", [128, 4, TT], BF16)
        hT = sb("hT", [128, 8, TT], BF16)
        qT = sb("qT", [128, 4, TT], BF16)
        kc = sb("kc", [128, 4, 2560], BF16)
        vc = sb("vc", [128, 4, 2560], BF16)
        oaT = sb("oaT", [128, 4, TT], BF16)
        obT = sb("obT", [128, 8, TT], BF16)
        VB = sb("VB", [32, 2048], BF16)
        HQ = sb("HQ", [128, 4, TT], F32)
        HF = sb("HF", [128, 4, TT], F32)
        HG = sb("HG", [128, 4, TT], F32)
        HE = sb("HE", [128, 4, TT], F32)
        QK8 = sb("QK8", [128, 8, TT], BF16)
        QD = QK8[:, 0:4, :]
        KD = QK8[:, 4:8, :]
        yT = QK8
        sz = szb
        SC = [SC0t[:], SC1t[:]]
        PT = [PT0t[:], PT1t[:]]
        RD = SC0t[:, 0:512]
        HFb = HF[:].rearrange("p a b -> p (a b)").bitcast(BF16)
        VA = [VA0t[:], VA1t[:]]
        SB8 = HFb.rearrange("p (c h v) -> p c h v", c=8, h=4)
        OBt = HG
        OSQt = HE
        IT = sb("IT", [128, 4, 512], BF16)
        RM = sb("RM", [128, TT], F32)
        KT8 = sb("KT8", [128, 4, 512], BF16)
        ATm8 = sb("ATm8", [128, 4, 256], BF16)
        KD2 = sb("KD2", [128, 4, TT], BF16)
        T1 = sb("T1", [128, 128], F32)

        PS = [ps("PS%d" % i, [128, 512], F32) for i in range(5)]
        PSS = ps("PSS", [128, 1024], F32)
        PSB = ps("PSB", [128, 1024], BF16)
        P_ACC = [0, 1]
        P_NUM, P_DEN, P_TR = 2, 3, 4

        sm = lambda col, n=1: small[:, col:col + n]

        def fsz(ap):
            n = 1
            for d in list(ap.shape)[1:]:
                n *= int(d)
            return n

        def dma(eng, out_ap, in_ap, reads, writes):
            nb = fsz(out_ap) * int(list(out_ap.shape)[0]) * 2
            return S.op(eng, lambda e: e.dma_start(out=out_ap, in_=in_ap), reads=reads, writes=writes, dma=True, nbytes=nb)

        def act(out_ap, in_ap, func, reads, writes, bias=None, scale=None, accum=None):
            kw = {}
            if bias is not None:
                kw["bias"] = bias
            if scale is not None:
                kw["scale"] = scale
            if accum is not None:
                kw["accum_out"] = accum
            if func == AF.Copy and (bias is not None or (scale is not None and not isinstance(scale, float))):
                func = AF.Identity
            return S.op("act", lambda e: e.activation(out=out_ap, in_=in_ap, func=func, **kw), reads=reads, writes=writes,
                        cost=230 + 0.72 * fsz(out_ap))

        def barrier(names):
            S.op("dve", lambda e: e.memset(T1[:, 127:128], 0.0), reads=[], writes=[B(n) for n in names] + [B("bar")])

        def tt(eng, out_ap, a, b, op, reads, writes):
            return S.op(eng, lambda e: e.tensor_tensor(out=out_ap, in0=a, in1=b, op=op), reads=reads, writes=writes, cost=120 + 1.05 * fsz(out_ap))

        def ts(eng, out_ap, a, s1, s2, op0, op1, reads, writes):
            if op1 is None:
                return S.op(eng, lambda e: e.tensor_scalar(out=out_ap, in0=a, scalar1=s1, scalar2=None, op0=op0), reads=reads, writes=writes, cost=150 + 1.05 * fsz(out_ap))
            return S.op(eng, lambda e: e.tensor_scalar(out=out_ap, in0=a, scalar1=s1, scalar2=s2, op0=op0, op1=op1), reads=reads, writes=writes, cost=150 + 1.05 * fsz(out_ap))

        def stt(out_ap, a, s, b, op0, op1, reads, writes):
            return S.op("dve", lambda e: e.scalar_tensor_tensor(out=out_ap, in0=a, scalar=s, in1=b, op0=op0, op1=op1), reads=reads, writes=writes, cost=150 + 1.05 * fsz(out_ap))

        def cp(eng, out_ap, in_ap, reads, writes):
            if eng == "act":
                return S.op(eng, lambda e: e.activation(out=out_ap, in_=in_ap, func=AF.Copy), reads=reads, writes=writes, cost=230 + 0.72 * fsz(out_ap))
            return S.op(eng, lambda e: e.tensor_copy(out=out_ap, in_=in_ap), reads=reads, writes=writes, cost=(120 if eng == "dve" else 250) + 1.05 * fsz(out_ap))

        def mm(out_ap, lhsT, rhs, start, stop, reads, writes, skip=False):
            c_ = max(fsz(rhs), 64) / 2.4 * (4 if lhsT.dtype == F32 else 1) + 12
            return S.op("pe", lambda e: e.matmul(out_ap, lhsT=lhsT, rhs=rhs, start=start, stop=stop, skip_group_check=skip), reads=reads, writes=writes, cost=c_)

        def tr(out_ap, in_ap, ident, reads, writes):
            return S.op("pe", lambda e: e.transpose(out_ap, in_=in_ap, identity=ident), reads=reads, writes=writes, cost=(260 if in_ap.dtype == F32 else 110))

        cB = B("const")
        S.op("pool", lambda e: e.memset(ident_f[:], 1.0), writes=[cB])
        S.op("pool", lambda e: e.affine_select(out=ident_f[:], in_=ident_f[:], pattern=[[1, 128]], compare_op=ALU.is_equal, fill=0.0, base=0, channel_multiplier=-1), reads=[cB], writes=[cB])
        cp("pool", ident_b[:], ident_f[:], [cB], [cB])
        S.op("pool", lambda e: e.memset(ones_b[:], 1.0), writes=[cB])
        S.op("pool", lambda e: e.memset(ones_f[:], 1.0), writes=[cB])
        S.op("pool", lambda e: e.memset(cmask[:], 1.0), writes=[cB])
        for hs_ in (0, 64):
            S.op("pool", lambda e, hs_=hs_: e.affine_select(out=cmask[hs_:hs_ + 64, :], in_=cmask[hs_:hs_ + 64, :], pattern=[[1, 64]], compare_op=ALU.is_ge, fill=0.0, base=0, channel_multiplier=-1), reads=[cB], writes=[cB])
        S.op("pool", lambda e: e.memset(small[:], 0.0), writes=[cB])
        S.op("pool", lambda e: e.memset(RM[:], 1.0), writes=[cB])
        S.op("pool", lambda e: e.memset(RM[:].rearrange("p (c t) -> p c t", t=64)[:, :, 0:1], 0.0), reads=[cB], writes=[cB])
        S.op("pool", lambda e: e.memset(Sst[:], 0.0), writes=[B("Sst")])
        S.op("pool", lambda e: e.memset(sm(EPSC), EPS), reads=[cB], writes=[cB])
        dma("pool", BM[:].rearrange("p a b -> p (a b)"), bm_in[:, :], [], [B("BM")])
        dma("sp", gateB[:], bgate_row.partition_broadcast(128), [], [B("gateB")])
        dma("sp", fgB[:], fg_row.partition_broadcast(128), [], [B("fgB")])
        smB = B("small_in")
        dma("sp", sm(GS, 8), ng_fm[:, :], [cB], [smB])
        dma("sp", sm(SH, 16), bada_fm[:, :], [cB], [smB])
        dma("sp", sm(GN), gn_col[:, :], [cB], [smB])
        dma("sp", sm(FLAG), flag_in[:, :], [cB], [smB])
        dma("sp", sm(CS, 8), cfm[:, :], [cB], [smB])
        dma("sp", sm(TMPC, 16), lb_fm[:, :], [cB], [smB])
        act(sm(CS, 8), sm(CS, 8), AF.Silu, [smB, cB], [cB])
        cs2v = small[:, CS2:CS2 + 16].rearrange("p (a b) -> p a b", b=2)
        cp("dve", cs2v[:, :, 0], sm(CS, 8), [cB], [cB])
        cp("dve", cs2v[:, :, 1], sm(CS, 8), [cB], [cB])

        def ws_buf(kind, g):
            if kind == "in" and g in (6, 7, 8, 9):
                return B("ws_early")
            if kind == "in" and g in (1, 2):
                return B("ws_mid")
            return B("ws_late")

        for g in (6, 8, 7, 9, 1, 2):
            dma("pool", win_bf[g, :, :], w_in[:, g * 512:(g + 1) * 512], [], [ws_buf("in", g)])

        def emit_late_conversions():
            for g in (0, 3, 4, 5, 10, 11, 12, 13, 14, 15):
                dma("pool", win_bf[g, :, :], w_in[:, g * 512:(g + 1) * 512], [], [ws_buf("in", g)])
            for g in range(2):
                dma("pool", wa_bf[g, :, :], w_a[:, g * 512:(g + 1) * 512], [], [ws_buf("a", g)])
                dma("pool", wb_bf[g, :, :], w_b[:, g * 512:(g + 1) * 512], [], [ws_buf("b", g)])
                dma("pool", wo_bf[g, :, :], w_o[:, g * 512:(g + 1) * 512], [], [ws_buf("o", g)])

        wada_sb = [HQ, HF, HG, HE]
        wadaB = B("hg_tmp")

        def wada_view(kc_):
            t = wada_sb[kc_ // 2]
            return t[:].rearrange("p a b -> p (a b)")[:, (kc_ % 2) * 1024:(kc_ % 2 + 1) * 1024]

        modps = PS[0]
        for part in range(3):
            for k2 in range(8):
                dma("sp", wada_view(k2), w_ada[k2 * 128:(k2 + 1) * 128, part * 1024:(part + 1) * 1024], [], [B("hg_tmp%d" % k2)])
            if part < 2:
                for cc in range(8):
                    for k2 in range(8):
                        mm(modps[:, 2 * cc:2 * cc + 2], wada_view(k2)[:, cc * 128:(cc + 1) * 128], cs2v[:, k2, :],
                           k2 == 0, k2 == 7, [B("hg_tmp%d" % k2), cB], [B("PS0")])
                dst = small[:, SH + 8 * part: SH + 8 * part + 8]
                src = modps[:, 0:16].rearrange("p (a b) -> p a b", b=2)[:, :, 0]
                tt("dve", dst, dst, src, ALU.add, [smB, cB, B("PS0")], [cB])
            else:
                for hh in range(2):
                    for k2 in range(8):
                        S.op("dve", lambda e, k2=k2: e.tensor_copy(out=T1[:], in_=small[:, CS + k2:CS + k2 + 1].to_broadcast([128, 128])), reads=[cB, B("T1")], writes=[B("T1")])
                        mm(PS[1][:, :], T1[:], wada_view(k2)[:, hh * 512:(hh + 1) * 512], k2 == 0, k2 == 7, [B("hg_tmp%d" % k2), B("T1")], [B("PS1")])
                    tt("dve", gateB[:, hh * 512:(hh + 1) * 512], gateB[:, hh * 512:(hh + 1) * 512], PS[1][:, :], ALU.add, [B("gateB"), B("PS1")], [B("gateB")])
        ts("dve", sm(LBV, 8), sm(LBV, 8), 1.0, None, ALU.add, None, [cB], [cB])
        tt("dve", sm(GS, 8), sm(GS, 8), sm(LBV, 8), ALU.mult, [cB], [cB])
        tt("dve", sm(LBV, 8), sm(TMPC, 8), sm(TMPC + 8, 8), ALU.subtract, [cB], [cB])
        act(sm(LBV, 8), sm(LBV, 8), AF.Sigmoid, [cB], [cB])
        ts("dve", sm(OML, 8), sm(LBV, 8), -1.0, 1.0, ALU.mult, ALU.add, [cB], [cB])
        ts("dve", sm(TMPC), sm(FLAG), -1.0, -NEG, ALU.add, ALU.mult, [cB], [cB])
        for J in range(4):
            n = 128 - 32 * J
            cp("dve", small[0:n, NBIAS + J:NBIAS + J + 1], small[0:n, TMPC:TMPC + 1], [cB], [cB])
        if debug:
            dma("sp", dbg["dbg_mod"][:, :], small[:, 0:16], [cB], [B("dbg_mod")])

        wstate = {"n": 0}

        WSRC = {"in": win_bf, "a": wa_bf, "b": wb_bf, "o": wo_bf}

        def load_w(kind, g, nk=8):
            i = wstate["n"] % NWB
            wstate["n"] += 1
            wb = B("wbuf%d" % i)
            dma("sp", wbuf[i][:, 0:nk, :], WSRC[kind][g, :, :].rearrange("(k p) n -> p k n", p=128), [ws_buf(kind, g)], [wb])
            return wbuf[i], wb

        xstate = {"n": 0}

        def load_x(src, t0):
            i = xstate["n"] % 2
            xstate["n"] += 1
            dma("sp", XB[i][:], src[t0:t0 + 128, :], [], [B("XB%d" % i)])
            return XB[i], B("XB%d" % i)

        def norm_tile(src, t0):
            hB = B("hT")
            for s in range(4):
                xt, xb = load_x(src, t0 + 128 * s)
                XNt = (SC0t, SC1t)[s % 2]
                xnB = B(("XN", "XN1")[s % 2])
                ssb_ = B("ss%d" % s)
                act(XNt[:].bitcast(BF16)[:, 0:D], xt[:], AF.Square, [xb], [xnB, ssb_], accum=sm(SS + s))
                act(sm(RSTD + s), sm(SS + s), AF.Ln, [ssb_, cB], [ssb_], bias=sm(EPSC), scale=1.0 / D)
                act(sm(RSTD + s), sm(RSTD + s), AF.Exp, [ssb_], [ssb_], scale=-0.5)
                ts("dve", XNt[:], xt[:], sm(RSTD + s), None, ALU.mult, None, [xb, ssb_], [xnB])
                for half in range(2):
                    pb = PS[P_TR] if half == 0 else PS[P_ACC[1]]
                    pbB = B("PS%d" % (P_TR if half == 0 else P_ACC[1]))
                    for k4 in range(4):
                        k2 = half * 4 + k4
                        tr(pb[:, k4 * 128:(k4 + 1) * 128], XNt[:, k2 * 128:(k2 + 1) * 128], ident_f[:], [xnB, cB], [pbB])
                    for k4 in range(4):
                        k2 = half * 4 + k4
                        if k4 % 2 == 0:
                            act(hT[:, k2, s * 128:(s + 1) * 128], pb[:, k4 * 128:(k4 + 1) * 128], AF.Identity, [pbB, cB], [hB],
                                bias=sm(SH + k2), scale=sm(GS + k2))
                        else:
                            ts("dve", hT[:, k2, s * 128:(s + 1) * 128], pb[:, k4 * 128:(k4 + 1) * 128], sm(GS + k2), sm(SH + k2), ALU.mult, ALU.add,
                               [pbB, cB], [hB])

        accn = {"n": 0}

        def next_acc():
            i = P_ACC[accn["n"] % 2]
            accn["n"] += 1
            return PS[i], B("PS%d" % i)

        def proj_fm(wt, wb, ncol_chunks, evac):
            for cc in range(ncol_chunks):
                pt, pB_ = next_acc()
                for k2 in range(8):
                    mm(pt[:, :], wt[:, k2, cc * 128:(cc + 1) * 128], hT[:, k2, :], k2 == 0, k2 == 7, [wb, B("hT")], [pB_])
                evac(cc, pt, pB_)

        def proj_tm(wt, wb, evac):
            for s in range(4):
                pt, pB_ = next_acc()
                for k2 in range(8):
                    mm(pt[:, :], hT[:, k2, s * 128:(s + 1) * 128], wt[:, k2, :], k2 == 0, k2 == 7, [wb, B("hT")], [pB_])
                evac(s, pt, pB_)

        def flat(t):
            return t[:].rearrange("p a b -> p (a b)")

        def hgrn_group(g, own):
            hq, hf, hg_, he = B("HQ"), B("HF"), B("HG"), B("HE")
            t1B = B("T1")
            PSBh = PS[P_TR][:].bitcast(BF16)
            psbhB = B("PS%d" % P_TR)
            wt, wb = load_w("in", 6 + g)
            proj_fm(wt, wb, 4, lambda cc, pt, pB_: act(HF[:, cc, :], pt[:, :], AF.Sigmoid, [pB_], [hf]))
            wt, wb = load_w("in", 8 + g)
            if own:
                proj_tm(wt, wb, lambda s, pt, pB_: act(IT[:, s, :], pt[:, :], AF.Copy, [pB_], [B("IT")]))
            else:
                proj_tm(wt, wb, lambda s, pt, pB_: act(IT[:, s, :], pt[:, :], AF.Copy, [pB_, cB], [B("IT")], scale=sm(FLAG)))
            if own:
                wt, wb = load_w("in", 4 + g)
                proj_fm(wt, wb, 4, lambda cc, pt, pB_: act(HQ[:, cc, :], pt[:, :], AF.Silu, [pB_], [hq]))
            halfB = {}
            for nm in ("HF", "HG", "HE"):
                for h_ in range(2):
                    halfB[(nm, h_)] = B("%s_h%d" % (nm, h_))
            S.op("dve", lambda e: e.memset(T1[:, 126:127], 0.0), reads=[hq], writes=list(halfB.values()) + [hf, hg_, he, B("bar2")], cost=80)
            chains = []
            for h_ in range(2):
                hs = slice(2 * h_, 2 * h_ + 2)
                fB, gB, eB = halfB[("HF", h_)], halfB[("HG", h_)], halfB[("HE", h_)]
                HFh, HGh, HEh, HQh = HF[:, hs, :], HG[:, hs, :], HE[:, hs, :], HQ[:, hs, :]
                fl = lambda ap: ap.rearrange("p a b -> p (a b)")
                bvh = HEh.rearrange("p a (c t) -> p (a c) t", t=64)
                gvh = HGh.rearrange("p a (c t) -> p (a c) t", t=64)
                KD2h, KDh, QDh = KD2[:, hs, :], KD[:, hs, :], QD[:, hs, :]
                st = []
                for hh in (2 * h_, 2 * h_ + 1):
                    hd = 4 * g + hh
                    st.append(lambda hh=hh, hd=hd, fB=fB: ts("dve", HF[:, hh, :], HF[:, hh, :], sm(OML + hd), sm(LBV + hd), ALU.mult, ALU.add, [fB, cB], [fB]))
                st.append(lambda HFh=HFh, HGh=HGh, fB=fB, gB=gB: act(fl(HGh), fl(HFh), AF.Ln, [fB], [gB]))
                st.append(lambda HFh=HFh, fB=fB: ts("dve", fl(HFh), fl(HFh), -1.0, 1.0, ALU.mult, ALU.add, [fB], [fB]))
                for hh in (2 * h_, 2 * h_ + 1):
                    st.append(lambda hh=hh, gB=gB, eB=eB: S.op("dve", lambda e: e.tensor_tensor_scan(out=HE[:, hh, :], data0=RM[:, :], data1=HG[:, hh, :], initial=0.0, op0=ALU.mult, op1=ALU.add),
                                                               reads=[gB, cB], writes=[eB], cost=100 + 2.1 * 512))
                st.append(lambda bvh=bvh, gvh=gvh, gB=gB, eB=eB: tt("dve", gvh, bvh, bvh[:, :, 31:32].to_broadcast([128, 16, 64]), ALU.subtract, [eB], [gB]))
                st.append(lambda bvh=bvh, eB=eB, h_=h_: act(T1[:, 16 * h_:16 * h_ + 16], bvh[:, :, 63], AF.Exp, [eB], [t1B]))
                st.append(lambda bvh=bvh, eB=eB, h_=h_: act(T1[:, 64 + 16 * h_:64 + 16 * h_ + 16], bvh[:, :, 31], AF.Exp, [eB], [t1B]))
                st.append(lambda bvh=bvh, gvh=gvh, gB=gB, eB=eB: tt("dve", bvh, gvh, gvh[:, :, 63:64].to_broadcast([128, 16, 64]), ALU.subtract, [gB, t1B], [eB]))
                st.append(lambda HEh=HEh, eB=eB: act(fl(HEh), fl(HEh), AF.Exp, [eB], [eB], scale=-1.0))
                st.append(lambda KD2h=KD2h, HFh=HFh, HEh=HEh, fB=fB, eB=eB: tt("dve", fl(KD2h), fl(HFh), fl(HEh), ALU.mult, [fB, eB], [B("KD2")]))
                if own:
                    st.append(lambda HEh=HEh, HGh=HGh, gB=gB, eB=eB: act(fl(HEh), fl(HGh), AF.Exp, [gB, B("KD2")], [eB], scale=-1.0))
                    st.append(lambda KDh=KDh, HFh=HFh, HEh=HEh, fB=fB, eB=eB: tt("dve", fl(KDh), fl(HFh), fl(HEh), ALU.mult, [fB, eB], [B("KD")]))
                    st.append(lambda HEh=HEh, HGh=HGh, gB=gB, eB=eB: act(fl(HEh), fl(HGh), AF.Exp, [gB, B("KD")], [eB]))
                    st.append(lambda QDh=QDh, HQh=HQh, HEh=HEh, eB=eB: tt("dve", fl(QDh), fl(HQh), fl(HEh), ALU.mult, [hq, eB], [B("QD")]))
                chains.append(st)
            n0 = len(chains[0])
            for k_ in range(n0 + 2):
                if k_ < n0:
                    chains[0][k_]()
                if 0 <= k_ - 2 < n0:
                    chains[1][k_ - 2]()
            S.op("dve", lambda e: e.memset(T1[:, 125:126], 0.0), reads=list(halfB.values()), writes=[hf, hg_, he, B("bar3")], cost=80)
            if own:
                wt, wb = load_w("in", 10 + g)
                proj_fm(wt, wb, 4, lambda cc, pt, pB_: act(HQ[:, cc, :], pt[:, :], AF.Silu, [pB_, B("QD")], [hq]))
            chs = [(ch, slice(ch * 64, ch * 64 + 64), ch // 2, slice((ch % 2) * 64, (ch % 2) * 64 + 64)) for ch in range(8)]
            for fill in range(2):
                for (ch, sl, sub, psl) in chs[4 * fill:4 * fill + 4]:
                    col0 = (sub % 2) * 512
                    for hh in range(4):
                        tr(PSBh[psl, col0 + hh * 128:col0 + (hh + 1) * 128], KD2[:, hh, sl], ident_b[:], [B("KD2"), cB], [psbhB])
                cp("act", KT8[:, 2 * fill:2 * fill + 2, :].rearrange("p a b -> p (a b)"), PSBh[:, 0:1024], [psbhB], [B("KT8")])
            if own and not SKIP.get("s6"):
                for fill in range(2):
                    ab, abB = next_acc()
                    for (ch, sl, sub, psl) in chs[4 * fill:4 * fill + 4]:
                        for hh in range(4):
                            c0_ = (sub % 2) * 256 + hh * 64
                            mm(ab[psl, c0_:c0_ + 64], KD[:, hh, sl], QD[:, hh, sl], True, True, [B("KD"), B("QD")], [abB])
                    tt("dve", ATm8[:, 2 * fill:2 * fill + 2, :].rearrange("p a (h t) -> p (a h) t", t=64), ab[:, :].rearrange("p (a t) -> p a t", t=64),
                       cmask[:].unsqueeze(1).to_broadcast([128, 8, 64]), ALU.mult, [abB, cB], [B("ATm8")])
            for (ch, sl, sub, psl) in chs:
                ubi = P_ACC[ch % 2]
                ub, ubB = PS[ubi], B("PS%d" % ubi)
                for hh in range(4):
                    mm(ub[:, hh * 128:(hh + 1) * 128], KT8[psl, sub, hh * 128:(hh + 1) * 128], IT[psl, sub, hh * 128:(hh + 1) * 128], True, True,
                       [B("KT8"), B("IT")], [ubB])
                for hh in range(4):
                    hd = 4 * g + hh
                    idx = hh * 8 + ch
                    if own:
                        act(SB8[:, ch, hh, :], Sst[:, hd, :], AF.Copy, [B("Sst"), t1B, hf], [hf], scale=T1[:, 64 + idx:65 + idx])
                    stt(Sst[:, hd, :], Sst[:, hd, :], T1[:, idx:idx + 1], ub[:, hh * 128:(hh + 1) * 128], ALU.mult, ALU.add, [B("Sst"), ubB, t1B], [B("Sst")])
            if own and not SKIP.get("s7"):
                for hh in range(4):
                    pt, pB_ = next_acc()
                    for (ch, sl, sub, psl) in chs:
                        mm(pt[:, sl], SB8[:, ch, hh, :], QD[:, hh, sl], True, False, [hf, B("QD")], [pB_])
                        mm(pt[:, sl], IT[psl, sub, hh * 128:(hh + 1) * 128], ATm8[psl, sub, hh * 64:(hh + 1) * 64], False, True,
                           [B("IT"), B("ATm8")], [pB_])
                    cp("dve", OBt[:, hh, :], pt[:, :], [pB_], [hg_])
                    if SKIP.get("s7b"):
                        continue
                    tt("dve", OSQt[:, hh, :], OBt[:, hh, :], OBt[:, hh, :], ALU.mult, [hg_], [he])
                    if SKIP.get("s7e"):
                        continue
                    st_, sB_ = next_acc()
                    for hf_ in range(2):
                        mm(st_[:, hf_ * 256:(hf_ + 1) * 256], ones_f[:, :], OSQt[:, hh, hf_ * 256:(hf_ + 1) * 256], True, True, [he, cB], [sB_])
                    if SKIP.get("s7c"):
                        continue
                    act(OSQt[:, hh, :], st_[:, :], AF.Ln, [sB_, cB], [he], bias=sm(EPSC), scale=1.0 / 128)
                    act(OSQt[:, hh, :], OSQt[:, hh, :], AF.Exp, [he], [he], scale=-0.5)
                    if SKIP.get("s7d"):
                        continue
                    stt(OBt[:, hh, :], OBt[:, hh, :], sm(GN), OSQt[:, hh, :], ALU.mult, ALU.mult, [hg_, he, cB], [hg_])
                    tt("dve", obT[:, 4 * g + hh, :], OBt[:, hh, :], HQ[:, hh, :], ALU.mult, [hg_, hq], [B("obT")])

        def attention(J):
            nB, dB, ssB, psbB = B("PS%d" % P_NUM), B("PS%d" % P_DEN), B("PSS"), B("PSB")
            kcB, vcB, qB = B("kc"), B("vc"), B("qT")
            for c in range(4):
                first = {0: True, 64: True}
                batches = []
                for p in range(3):
                    def build_v(p=p):
                        if p == 0:
                            ktiles = [(2048 + 128 * n, 1, 128) for n in (-1, 3, 0, 1, 2)]
                        elif p == 1:
                            ktiles = []
                            for r in range(4):
                                ktiles += [(2048 + r, 4, 128), (1536 + r, 4, 128)]
                        else:
                            ktiles = [(r, 16, 128) for r in range(16)]
                        for rnd in range(0, len(ktiles), 8):
                            grp = ktiles[rnd:rnd + 8]
                            for i_, (c0, stp, nk) in enumerate(grp):
                                tr(PSB[:, i_ * 128:(i_ + 1) * 128], vc[:, c, c0:c0 + stp * (nk - 1) + 1:stp], ident_b[:], [vcB, cB], [psbB])
                            cp("act", VA[p % 2][:, rnd * 128:(rnd + len(grp)) * 128], PSB[:, 0:len(grp) * 128], [psbB], [B("VA%d" % (p % 2))])
                        if p == 2:
                            for rnd in range(0, 16, 8):
                                for i_ in range(8):
                                    r = rnd + i_
                                    tr(PSB[0:32, i_ * 128:(i_ + 1) * 128], vc[:, c, 2048 + r:2048 + r + 16 * 31 + 1:16], ident_b[:], [vcB, cB], [psbB])
                                cp("act", VB[:, rnd * 128:(rnd + 8) * 128], PSB[0:32, 0:1024], [psbB], [B("VB")])
                    for hs in (0, 64):
                        batches.append((p, hs, build_v if hs == 0 else None))

                def qk_and_softmax(bi, p, hs, build_v):
                    hsl = slice(hs, hs + 64)
                    h = 2 * c + hs // 64
                    bmh = BM[:, p * 8 + h, :]
                    sc, scB = SC[bi % 2], B("SC%d" % (bi % 2))
                    pt, ptB = PT[bi % 2], B("PT%d" % (bi % 2))
                    if p == 0:
                        blocks = [(-1, 0, 128), (3, 128, 128), (0, 256, 256), (1, 512, 256), (2, 768, 256)]
                        for (n, col, w) in blocks:
                            q0 = 0 if n == -1 else 128 * n
                            kc0 = 2048 + 128 * n
                            mm(PSS[:, col:col + w], kc[hsl, c, kc0:kc0 + 128], qT[hsl, c, q0:q0 + w], True, True, [kcB, qB], [ssB])
                        tt("dve", sc[:, 0:128], PSS[:, 0:128], bmh[:, 128:256], ALU.add, [ssB, B("BM")], [scB])
                        tt("dve", sc[:, 128:256], PSS[:, 128:256], bmh[:, 0:128], ALU.add, [ssB, B("BM")], [scB])
                        tt("dve", sc[:, 256:1024].rearrange("p (a b) -> p a b", b=256), PSS[:, 256:1024].rearrange("p (a b) -> p a b", b=256),
                           bmh.unsqueeze(1).to_broadcast([128, 3, 256]), ALU.add, [ssB, B("BM")], [scB])
                        if J == 0:
                            act(pt[:, 0:128], sc[:, 0:128], AF.Exp, [scB, cB], [ptB], bias=sm(NBIAS))
                            act(pt[:, 128:1024], sc[:, 128:1024], AF.Exp, [scB], [ptB])
                        else:
                            act(pt[:, :], sc[:, :], AF.Exp, [scB], [ptB])
                    elif p == 1:
                        for r in range(4):
                            qv = qT[hsl, c, r:r + 4 * 127 + 1:4]
                            mm(PSS[:, r * 256:r * 256 + 128], kc[hsl, c, 2048 + r:2048 + r + 4 * 127 + 1:4], qv, True, True, [kcB, qB], [ssB])
                            mm(PSS[:, r * 256 + 128:r * 256 + 256], kc[hsl, c, 1536 + r:1536 + r + 4 * 127 + 1:4], qv, True, True, [kcB, qB], [ssB])
                        tt("dve", sc[:, :].rearrange("p (a b) -> p a b", b=256), PSS[:, :].rearrange("p (a b) -> p a b", b=256),
                           bmh.unsqueeze(1).to_broadcast([128, 4, 256]), ALU.add, [ssB, B("BM")], [scB])
                        if J == 0:
                            scv = sc[:, :].rearrange("p (a t b) -> p a t b", t=2, b=128)
                            ptv = pt[:, :].rearrange("p (a t b) -> p a t b", t=2, b=128)
                            act(ptv[:, :, 0, :], scv[:, :, 0, :], AF.Exp, [scB], [ptB])
                            act(ptv[:, :, 1, :], scv[:, :, 1, :], AF.Exp, [scB, cB], [ptB], bias=sm(NBIAS))
                        else:
                            act(pt[:, :], sc[:, :], AF.Exp, [scB], [ptB])
                    else:
                        for r in range(16):
                            qv = qT[hsl, c, r:r + 16 * 31 + 1:16]
                            mm(PSS[:, r * 32:(r + 1) * 32], kc[hsl, c, r:r + 16 * 127 + 1:16], qv, True, True, [kcB, qB], [ssB])
                        for r in range(16):
                            qv = qT[hsl, c, r:r + 16 * 31 + 1:16]
                            mm(PSS[0:32, 512 + r * 32:512 + (r + 1) * 32], kc[hsl, c, 2048 + r:2048 + r + 16 * 31 + 1:16], qv, True, True, [kcB, qB], [ssB])
                        tt("dve", sc[:, 0:512].rearrange("p (a b) -> p a b", b=32), PSS[:, 0:512].rearrange("p (a b) -> p a b", b=32),
                           bmh[:, 128:160].unsqueeze(1).to_broadcast([128, 16, 32]), ALU.add, [ssB, B("BM")], [scB])
                        tt("dve", sc[0:32, 512:1024].rearrange("p (a b) -> p a b", b=32), PSS[0:32, 512:1024].rearrange("p (a b) -> p a b", b=32),
                           bmh[0:32, 0:32].unsqueeze(1).to_broadcast([32, 16, 32]), ALU.add, [ssB, B("BM")], [scB])
                        if J < 4:
                            act(pt[:, 0:512], sc[:, 0:512], AF.Exp, [scB, cB], [ptB], bias=sm(NBIAS + J))
                        else:
                            act(pt[:, 0:512], sc[:, 0:512], AF.Exp, [scB], [ptB])
                        act(pt[0:32, 512:1024], sc[0:32, 512:1024], AF.Exp, [scB], [ptB])

                def pv(bi, p, hs, last):
                    hsl = slice(hs, hs + 64)
                    pt, ptB = PT[bi % 2], B("PT%d" % (bi % 2))
                    items = []
                    if p == 0:
                        blocks = [(-1, 0, 128), (3, 128, 128), (0, 256, 256), (1, 512, 256), (2, 768, 256)]
                        for i_, (n, col, w) in enumerate(blocks):
                            q0 = 0 if n == -1 else 128 * n
                            items.append((VA[0][:, i_ * 128 + hs:i_ * 128 + hs + 64], ones_b[:, 0:64], pt[:, col:col + w], slice(q0, q0 + w), [B("VA0")]))
                    elif p == 1:
                        for r in range(4):
                            for t_ in range(2):
                                i_ = 2 * r + t_
                                items.append((VA[1][:, i_ * 128 + hs:i_ * 128 + hs + 64], ones_b[:, 0:64], pt[:, i_ * 128:(i_ + 1) * 128],
                                              slice(r, r + 4 * 127 + 1, 4), [B("VA1")]))
                    else:
                        for r in range(16):
                            items.append((VA[0][:, r * 128 + hs:r * 128 + hs + 64], ones_b[:, 0:64], pt[:, r * 32:(r + 1) * 32],
                                          slice(r, r + 16 * 31 + 1, 16), [B("VA0")]))
                        for r in range(16):
                            items.append((VB[:, r * 128 + hs:r * 128 + hs + 64], ones_b[0:32, 0:64], pt[0:32, 512 + r * 32:512 + (r + 1) * 32],
                                          slice(r, r + 16 * 31 + 1, 16), [B("VB")]))
                    for k_, (vap, oap, rhs, qsl, vb) in enumerate(items):
                        st_ = first[hs]
                        first[hs] = False
                        sp_ = last and (k_ == len(items) - 1)
                        mm(PS[P_NUM][hsl, qsl], vap, rhs, st_, sp_, vb + [ptB], [nB], skip=True)
                        mm(PS[P_DEN][hsl, qsl], oap, rhs, st_, sp_, [cB, ptB], [dB], skip=True)

                nb_ = len(batches)
                for bi in range(nb_ + 1):
                    if bi < nb_:
                        p, hs, bv_ = batches[bi]
                        if bv_ is not None:
                            bv_()
                        qk_and_softmax(bi, p, hs, bv_)
                    if bi >= 1:
                        p, hs, _ = batches[bi - 1]
                        pv(bi - 1, p, hs, last=(p == 2))
                S.op("dve", lambda e: e.reciprocal(out=RD, in_=PS[P_DEN][:, :]), reads=[dB], writes=[B("SC0")], cost=650)
                tt("dve", RD, PS[P_NUM][:, :], RD, ALU.mult, [nB, B("SC0")], [B("SC0")])
                tt("dve", oaT[:, c, :], RD, sz[:, c, :], ALU.mult, [B("SC0"), B("sz")], [B("oaT")])

        def merge_and_out(J):
            hq, hf = B("HQ"), B("HF")
            for half in range(2):
                wt, wb = load_w("in", 12 + half)
                proj_fm(wt, wb, 4, lambda cc, pt, pB_: act(HQ[:, cc, :], pt[:, :], AF.Sigmoid, [pB_], [hq]))
                wt, wb = load_w("in", 14 + half)
                proj_fm(wt, wb, 4, lambda cc, pt, pB_: act(HF[:, cc, :], pt[:, :], AF.Sigmoid, [pB_], [hf]))
                wta, wba = load_w("a", half, 4)
                for cc in range(4):
                    pt, pB_ = next_acc()
                    for k2 in range(4):
                        mm(pt[:, :], wta[:, k2, cc * 128:(cc + 1) * 128], oaT[:, k2, :], k2 == 0, k2 == 3, [wba, B("oaT")], [pB_])
                    tt("dve", HQ[:, cc, :], HQ[:, cc, :], pt[:, :], ALU.mult, [hq, pB_], [hq])
                wtb, wbb = load_w("b", half)
                for cc in range(4):
                    pt, pB_ = next_acc()
                    for k2 in range(8):
                        mm(pt[:, :], wtb[:, k2, cc * 128:(cc + 1) * 128], obT[:, k2, :], k2 == 0, k2 == 7, [wbb, B("obT")], [pB_])
                    tt("dve", HF[:, cc, :], HF[:, cc, :], pt[:, :], ALU.mult, [hf, pB_], [hf])
                    tt("dve", yT[:, 4 * half + cc, :], HF[:, cc, :], HQ[:, cc, :], ALU.add, [hf, hq], [B("yT")])
            wts = [load_w("o", n) for n in range(2)]
            for s in range(4):
                xt, xb = load_x(xo, J * TT + 128 * s)
                XNt = (SC0t, SC1t)[s % 2]
                xnB = B(("XN", "XN1")[s % 2])
                ssb_ = B("ss%d" % s)
                for n in range(2):
                    wt, wb = wts[n]
                    pt, pB_ = next_acc()
                    for k2 in range(8):
                        mm(pt[:, :], yT[:, k2, s * 128:(s + 1) * 128], wt[:, k2, :], k2 == 0, k2 == 7, [wb, B("yT")], [pB_])
                    tt("dve", XNt[:, n * 512:(n + 1) * 512], pt[:, :], gateB[:, n * 512:(n + 1) * 512], ALU.mult, [pB_, B("gateB")], [xnB])
                tt("dve", XNt[:], XNt[:], xt[:], ALU.add, [xnB, xb], [xnB])
                act(PT[s % 2], XNt[:], AF.Square, [xnB], [B("PT%d" % (s % 2)), ssb_], accum=sm(SS + s))
                act(sm(RSTD + s), sm(SS + s), AF.Ln, [ssb_, cB], [ssb_], bias=sm(EPSC), scale=1.0 / D)
                act(sm(RSTD + s), sm(RSTD + s), AF.Exp, [ssb_], [ssb_], scale=-0.5)
                stt(xt[:], XNt[:], sm(RSTD + s), fgB[:], ALU.mult, ALU.mult, [xnB, ssb_, B("fgB"), xb], [xb])
                r = dma("sp", out[J * TT + 128 * s:J * TT + 128 * (s + 1), :], xt[:], [xb], [B("out")])
                out_dmas.append(r)

        out_dmas = []

        barrier(["hg_tmp%d" % i for i in range(8)] + ["HQ", "HF", "HG", "HE", "SC0", "SC1", "PT0", "PT1", "RD", "VA0", "VA1", "sz", "QD", "KD", "yT", "XN", "XN1"])
        first_pre = NT - n_pre_tiles
        for j in range(first_pre, NT):
            norm_tile(xp, j * TT)
            if j >= 4:
                pos = (j - 4) * TT
                wt, wb = load_w("in", 1)
                proj_fm(wt, wb, 4, lambda cc, pt, pB_, pos=pos: cp("act", kc[:, cc, pos:pos + TT], pt[:, :], [pB_], [B("kc")]))
                wt, wb = load_w("in", 2)
                proj_fm(wt, wb, 4, lambda cc, pt, pB_, pos=pos: cp("act", vc[:, cc, pos:pos + TT], pt[:, :], [pB_], [B("vc")]))
            for g in range(2):
                hgrn_group(g, own=False)
            if j == first_pre:
                emit_late_conversions()
        if n_pre_tiles == 0:
            emit_late_conversions()
        if debug:
            dma("sp", dbg["dbg_S"][:, :], Sst[:].rearrange("p a b -> p (a b)"), [B("Sst")], [B("dbg_S")])
        ALLB = ["HQ", "HF", "HG", "HE", "SC0", "SC1", "PT0", "PT1", "VA0", "VA1", "sz", "QD", "KD", "KD2", "yT", "XN", "XN1"] + ["hg_tmp%d" % i for i in range(8)]
        for J in range(n_own_tiles):
            norm_tile(xo, J * TT)
            if debug and J == 0:
                dma("sp", dbg["dbg_hT"][:, :], hT[:].rearrange("p a b -> p (a b)"), [B("hT")], [B("dbg_hT")])
            barrier(ALLB)
            wt, wb = load_w("in", 0)
            proj_fm(wt, wb, 4, lambda cc, pt, pB_: act(qT[:, cc, :], pt[:, :], AF.Copy, [pB_], [B("qT")], scale=0.125))
            wt, wb = load_w("in", 1)
            proj_fm(wt, wb, 4, lambda cc, pt, pB_: cp("act", kc[:, cc, 2048:2560], pt[:, :], [pB_], [B("kc")]))
            wt, wb = load_w("in", 2)
            proj_fm(wt, wb, 4, lambda cc, pt, pB_: cp("act", vc[:, cc, 2048:2560], pt[:, :], [pB_], [B("vc")]))
            wt, wb = load_w("in", 3)
            proj_fm(wt, wb, 4, lambda cc, pt, pB_: act(sz[:, cc, :], pt[:, :], AF.Silu, [pB_], [B("sz")]))
            if debug and J == 0:
                dma("sp", dbg["dbg_q"][:, :], qT[:].rearrange("p a b -> p (a b)"), [B("qT")], [B("dbg_q")])
            S.begin_capture()
            attention(J)
            if J < n_own_tiles - 1:
                for blk in range(4):
                    cp("pool", kc[:, :, blk * 512:(blk + 1) * 512], kc[:, :, (blk + 1) * 512:(blk + 2) * 512], [B("kc")], [B("kc")])
                    cp("pool", vc[:, :, blk * 512:(blk + 1) * 512], vc[:, :, (blk + 1) * 512:(blk + 2) * 512], [B("vc")], [B("vc")])
            stA = S.end_capture()
            S.begin_capture()
            for g in range(2):
                hgrn_group(g, own=True)
            stB = S.end_capture()
            if MERGE:
                S.merge([stA, stB])
            else:
                S.merge([stA])
                S.merge([stB])
            barrier(ALLB)
            if debug and J == 0:
                dma("sp", dbg["dbg_oa"][:, :], oaT[:].rearrange("p a b -> p (a b)"), [B("oaT")], [B("dbg_oa")])
                dma("sp", dbg["dbg_ob"][:, :], obT[:].rearrange("p a b -> p (a b)"), [B("obT")], [B("dbg_ob")])
            merge_and_out(J)
            barrier(ALLB)
            if debug and J == 0:
                dma("sp", dbg["dbg_y"][:, :], yT[:].rearrange("p a b -> p (a b)"), [B("yT")], [B("dbg_y")])

        S.finalize()
        finals = []
        seen = set()
        for b_ in bufs.values():
            if b_.dsem is not None and (b_.name == "out" or b_.name.startswith("dbg")):
                finals.append((b_.dsem, b_.dcnt))
        for b_ in bufs.values():
            if b_.dsem is not None and not (b_.name == "out" or b_.name.startswith("dbg")):
                finals.append((b_.dsem, b_.dcnt))
        with nc.Block() as block:
            @block.tensor
            def _(e):
                S.emit_engine("pe", e)

            @block.scalar
            def _(e):
                S.emit_engine("act", e)

            @block.vector
            def _(e):
                S.emit_engine("dve", e)

            @block.gpsimd
            def _(e):
                S.emit_engine("pool", e)

            @block.sync
            def _(e):
                S.emit_engine("sp", e, final_waits=finals)
    return nc, len(S.ins)


def _t5_bucket_np(dist):
    n_buckets, max_distance = 32, 2048
    max_exact = n_buckets // 2
    n = dist.astype(np.float32)
    large = max_exact + (np.log(np.maximum(n, np.float32(1.0)) / np.float32(max_exact))
                         / np.float32(math.log(max_distance / max_exact))
                         * np.float32(n_buckets - max_exact)).astype(np.int32)
    large = np.minimum(large, n_buckets - 1)
    return np.where(dist < max_exact, dist, large)


def _bias_tiles(rel_bias):
    kl = np.arange(128)[:, None]
    qq = np.arange(256)[None, :]
    delta = qq - kl
    valid = (delta >= 0) & (delta <= 128)
    outp = np.full((128, 3, 8, 256), NEG, dtype=np.float32)
    for p, d in enumerate((1, 4, 16)):
        bucket = _t5_bucket_np(np.clip(delta, 0, None) * d)
        g = rel_bias[bucket]
        for h in range(8):
            outp[:, p, h, :] = np.where(valid, g[:, :, h], np.float32(NEG))
    return np.ascontiguousarray(outp.reshape(128, 24 * 256))


def _fm(v, nchunk):
    return np.ascontiguousarray(np.asarray(v, dtype=np.float32).reshape(nchunk, 128).T)


_PROGRAM = {}


def kernel(x, c, w_ada, b_ada, norm_g, w_in, hgrn_onorm_g, w_branch_a, w_branch_b,
           w_out, rel_bias, hgrn_lb, final_g, _debug=False, _tiles=None):
    x = np.asarray(x, dtype=np.float32)
    key = (bool(_debug), _tiles)
    if key not in _PROGRAM:
        if _tiles is None:
            _PROGRAM[key] = build_program(debug=_debug)
        else:
            _PROGRAM[key] = build_program(n_own_tiles=_tiles[0], n_pre_tiles=_tiles[1], debug=_debug)
    nc, _ = _PROGRAM[key]
    b_ada0 = np.asarray(b_ada, dtype=np.float32)[0]
    shared = {
        "w_ada": np.ascontiguousarray(np.asarray(w_ada, dtype=np.float32)[0]),
        "bada_fm": _fm(b_ada0[:2048], 16),
        "bgate_row": np.ascontiguousarray(b_ada0[2048:].reshape(1, D)),
        "ng_fm": _fm(np.asarray(norm_g)[0], 8),
        "w_in": np.ascontiguousarray(np.asarray(w_in, dtype=np.float32)[0]),
        "gn_col": np.ascontiguousarray(np.asarray(hgrn_onorm_g, dtype=np.float32)[0].reshape(128, 1)),
        "w_a": np.ascontiguousarray(np.asarray(w_branch_a, dtype=np.float32)[0]),
        "w_b": np.ascontiguousarray(np.asarray(w_branch_b, dtype=np.float32)[0]),
        "w_o": np.ascontiguousarray(np.asarray(w_out, dtype=np.float32)[0]),
        "bm": _bias_tiles(np.asarray(rel_bias, dtype=np.float32)),
        "lb_fm": np.ascontiguousarray(np.concatenate([_fm(np.asarray(hgrn_lb)[0], 8), _fm(np.asarray(hgrn_lb)[1], 8)], axis=1)),
        "fg_row": np.ascontiguousarray(np.asarray(final_g, dtype=np.float32).reshape(1, D)),
    }
    in_maps = []
    for core in range(8):
        b, half = core // 2, core % 2
        m = dict(shared)
        m["xo"] = np.ascontiguousarray(x[b, half * TOWN:(half + 1) * TOWN])
        m["xp"] = np.ascontiguousarray(x[b, 0:TOWN]) if half == 1 else np.zeros((TOWN, D), np.float32)
        m["cfm"] = _fm(np.asarray(c, dtype=np.float32)[b], 8)
        m["flag"] = np.full((128, 1), float(half), np.float32)
        in_maps.append(m)
    res = run_bass_kernel_spmd(nc, in_maps, core_ids=list(range(8)))
    outp = np.empty((NB, SEQ, D), np.float32)
    for core in range(8):
        b, half = core // 2, core % 2
        outp[b, half * TOWN:(half + 1) * TOWN] = res.results[core]["out"]
    if _debug:
        return outp, res.results
    return outp
```

```python
import math
from contextlib import ExitStack

import numpy as np
import concourse.bass as bass
import concourse.mybir as mybir
from concourse.bass_utils import run_bass_kernel_spmd

F32 = mybir.dt.float32
BF16 = mybir.dt.bfloat16
ALU = mybir.AluOpType
AF = mybir.ActivationFunctionType

D = 1024
SEQ = 8192
NB = 4
TOWN = 4096
TT = 512
NT = TOWN // TT
NEG = -30000.0
EPS = 1e-6
DEBUG = False
SKIP = {}
MERGE = True
SIMLAT = 60.0
PESWITCH = 1200.0


class Buf:
    __slots__ = ("name", "lastw", "readers", "dsem", "dcnt")

    def __init__(self, name):
        self.name = name
        self.lastw = None
        self.readers = []
        self.dsem = None
        self.dcnt = 0


class Instr:
    __slots__ = ("id", "eng", "fn", "deps", "is_dma", "dsem", "dval", "lidx", "marked", "seq", "waits")


class Sched:
    ENGS = ("pe", "act", "dve", "pool", "sp")

    def __init__(self, nc, stack):
        self.nc = nc
        self.stack = stack
        self.ins = []
        self.q = {e: [] for e in self.ENGS}
        self.esem = {e: stack.enter_context(nc.semaphore("s_" + e)) for e in self.ENGS if e != "sp"}
        self.capture = None

    def begin_capture(self):
        self.capture = []

    def end_capture(self):
        c = self.capture
        self.capture = None
        return c

    def merge(self, streams):
        eng_free = {e: 0.0 for e in self.ENGS}
        bw, br = {}, {}
        idx = [0] * len(streams)
        order = []
        last_pe = -1
        total = sum(len(st) for st in streams)
        while len(order) < total:
            best = None
            for si, st in enumerate(streams):
                if idx[si] >= len(st):
                    continue
                eng, fn, reads, writes, dma, cost, nbytes = st[idx[si]]
                t = eng_free[eng]
                for b in reads:
                    t = max(t, bw.get(b, 0.0))
                for b in writes:
                    t = max(t, bw.get(b, 0.0), br.get(b, 0.0))
                if eng == "pe" and last_pe >= 0 and last_pe != si:
                    t += PESWITCH
                key = (t, idx[si] / float(len(st)))
                if best is None or key < best[0]:
                    best = (key, si, t)
            _, si, t = best
            o = streams[si][idx[si]]
            idx[si] += 1
            eng, fn, reads, writes, dma, cost, nbytes = o
            c = cost if cost else 300.0
            if eng == "pe":
                last_pe = si
            if dma:
                eng_free[eng] = t + 60.0
                done = t + 2000.0 + nbytes / 200.0
            else:
                eng_free[eng] = t + c
                done = t + c + SIMLAT
            for b in reads:
                br[b] = max(br.get(b, 0.0), done)
            for b in writes:
                bw[b] = done
            order.append(o)
        for (eng, fn, reads, writes, dma, cost, nbytes) in order:
            self.op(eng, fn, reads, writes, dma)

    def op(self, eng, fn, reads=(), writes=(), dma=False, cost=None, nbytes=0):
        if self.capture is not None:
            self.capture.append((eng, fn, tuple(reads), tuple(writes), dma, cost, nbytes))
            return None
        i = Instr()
        i.id = len(self.ins)
        i.eng = eng
        i.fn = fn
        i.is_dma = dma
        i.marked = False
        i.seq = 0
        i.waits = []
        deps = set()
        for b in reads:
            if b.lastw is not None:
                deps.add(b.lastw)
        for b in writes:
            if b.lastw is not None:
                deps.add(b.lastw)
            for r in b.readers:
                deps.add(r)
        i.deps = deps
        if dma:
            tgt = writes[0] if writes else reads[0]
            if tgt.dsem is None:
                tgt.dsem = self.stack.enter_context(self.nc.semaphore("d_" + tgt.name))
            tgt.dcnt += 16
            i.dsem = tgt.dsem
            i.dval = tgt.dcnt
        for b in reads:
            b.readers.append(i.id)
        for b in writes:
            b.lastw = i.id
            b.readers = []
        i.lidx = len(self.q[eng])
        self.q[eng].append(i)
        self.ins.append(i)
        return i

    def finalize(self):
        for e in self.ENGS:
            waited = {}
            for i in self.q[e]:
                for d in sorted(i.deps):
                    di = self.ins[d]
                    if di.is_dma:
                        key = ("d", id(di.dsem))
                        if waited.get(key, 0) >= di.dval:
                            continue
                        waited[key] = di.dval
                        i.waits.append((di.dsem, di.dval, None))
                    else:
                        if di.eng == e and e == "pe":
                            continue
                        key = ("e", di.eng)
                        if waited.get(key, -1) >= di.lidx:
                            continue
                        waited[key] = di.lidx
                        di.marked = True
                        i.waits.append((None, None, di))
        for e in self.ENGS:
            c = 0
            for i in self.q[e]:
                if i.marked:
                    c += 1
                    i.seq = c

    def emit_engine(self, e, engine, final_waits=()):
        for i in self.q[e]:
            for (sem, val, di) in i.waits:
                if di is None:
                    engine.wait_ge(sem, val)
                else:
                    engine.wait_ge(self.esem[di.eng], di.seq)
            r = i.fn(engine)
            if i.is_dma:
                r.then_inc(i.dsem, 16)
            elif i.marked:
                r.then_inc(self.esem[e], 1)
        for (sem, val) in final_waits:
            engine.wait_ge(sem, val)


def build_program(n_own_tiles=NT, n_pre_tiles=NT, debug=False):
    nc = bass.Bass("TRN2", target_bir_lowering=False)
    dt_in = lambda name, shape, dt=F32: nc.dram_tensor(name, shape, dt, kind="ExternalInput").ap()
    xo = dt_in("xo", [TOWN, D])
    xp = dt_in("xp", [TOWN, D])
    cfm = dt_in("cfm", [128, 8])
    w_ada = dt_in("w_ada", [D, 3 * D])
    bada_fm = dt_in("bada_fm", [128, 16])
    bgate_row = dt_in("bgate_row", [1, D])
    ng_fm = dt_in("ng_fm", [128, 8])
    w_in = dt_in("w_in", [D, 8192])
    gn_col = dt_in("gn_col", [128, 1])
    w_a = dt_in("w_a", [512, D])
    w_b = dt_in("w_b", [D, D])
    w_o = dt_in("w_o", [D, D])
    bm_in = dt_in("bm", [128, 24 * 256])
    lb_fm = dt_in("lb_fm", [128, 16])
    fg_row = dt_in("fg_row", [1, D])
    flag_in = dt_in("flag", [128, 1])
    out = nc.dram_tensor("out", [TOWN, D], F32, kind="ExternalOutput").ap()
    win_bf = nc.dram_tensor("win_bf", [16, D, 512], BF16, kind="Internal").ap()
    wa_bf = nc.dram_tensor("wa_bf", [2, 512, 512], BF16, kind="Internal").ap()
    wb_bf = nc.dram_tensor("wb_bf", [2, D, 512], BF16, kind="Internal").ap()
    wo_bf = nc.dram_tensor("wo_bf", [2, D, 512], BF16, kind="Internal").ap()
    dbg = {}
    if debug:
        for nm, shp, dt in [("dbg_hT", [128, 8 * 512], BF16), ("dbg_q", [128, 4 * 512], BF16),
                            ("dbg_oa", [128, 4 * 512], BF16), ("dbg_ob", [128, 8 * 512], BF16),
                            ("dbg_y", [128, 8 * 512], BF16), ("dbg_mod", [128, 16], F32),
                            ("dbg_S", [128, 8 * 128], F32)]:
            dbg[nm] = nc.dram_tensor(nm, shp, dt, kind="ExternalOutput").ap()

    with ExitStack() as st:
        S = Sched(nc, st)
        sb = lambda name, shape, dt: st.enter_context(nc.sbuf_tensor(name, shape, dt))
        ps = lambda name, shape, dt: st.enter_context(nc.psum_tensor(name, shape, dt))
        bufs = {}

        def B(name):
            if name not in bufs:
                bufs[name] = Buf(name)
            return bufs[name]

        BM = sb("BM", [128, 24, 256], BF16)
        ident_f = sb("ident_f", [128, 128], F32)
        ident_b = sb("ident_b", [128, 128], BF16)
        ones_b = sb("ones_b", [128, 64], BF16)
        ones_f = sb("ones_f", [128, 128], F32)
        cmask = sb("cmask", [128, 64], F32)
        gateB = sb("gateB", [128, D], F32)
        fgB = sb("fgB", [128, D], F32)
        small = sb("small", [128, 128], F32)
        GS, SH, LBV, OML, GN, FLAG, NBIAS, EPSC, SS, RSTD, CS = 0, 8, 16, 24, 32, 33, 34, 38, 39, 43, 47
        CS2 = 56
        TMPC = 72
        Sst = sb("Sst", [128, 8, 128], F32)
        NWB = 3
        wbuf = [sb("wbuf%d" % i, [128, 8, 512], BF16) for i in range(NWB)]
        XB = [sb("XB%d" % i, [128, D], F32) for i in range(2)]
        SC0t = sb("SC0t", [128, 1024], F32)
        SC1t = sb("SC1t", [128, 1024], F32)
        XN = SC0t
        JUNK = SC0t[:].bitcast(BF16)[:, 0:D]
        PT0t = sb("PT0t", [128, 1024], BF16)
        PT1t = sb("PT1t", [128, 1024], BF16)
        VA0t = sb("VA0t", [128, 2048], BF16)
        VA1t = sb("VA1t", [128, 1024], BF16)
        szb = sb("szb", [128, 4, TT], BF16)
        hT = sb("hT", [128, 8, TT], BF16)
        qT = sb("qT", [128, 4, TT], BF16)
        kc = sb("kc", [128, 4, 2560], BF16)
        vc = sb("vc", [128, 4, 2560], BF16)
        oaT = sb("oaT", [128, 4, TT], BF16)
        obT = sb("obT", [128, 8, TT], BF16)
        VB = sb("VB", [32, 2048], BF16)
        HQ = sb("HQ", [128, 4, TT], F32)
        HF = sb("HF", [128, 4, TT], F32)
        HG = sb("HG", [128, 4, TT], F32)
        HE = sb("HE", [128, 4, TT], F32)
        QK8 = sb("QK8", [128, 8, TT], BF16)
        QD = QK8[:, 0:4, :]
        KD = QK8[:, 4:8, :]
        yT = QK8
        sz = szb
        SC = [SC0t[:], SC1t[:]]
        PT = [PT0t[:], PT1t[:]]
        RD = SC0t[:, 0:512]
        HFb = HF[:].rearrange("p a b -> p (a b)").bitcast(BF16)
        VA = [VA0t[:], VA1t[:]]
        SB8 = HFb.rearrange("p (c h v) -> p c h v", c=8, h=4)
        OBt = HG
        OSQt = HE
        IT = sb("IT", [128, 4, 512], BF16)
        RM = sb("RM", [128, TT], F32)
        KT8 = sb("KT8", [128, 4, 512], BF16)
        ATm8 = sb("ATm8", [128, 4, 256], BF16)
        KD2 = sb("KD2", [128, 4, TT], BF16)
        T1 = sb("T1", [128, 128], F32)

        PS = [ps("PS%d" % i, [128, 512], F32) for i in range(5)]
        PSS = ps("PSS", [128, 1024], F32)
        PSB = ps("PSB", [128, 1024], BF16)
        P_ACC = [0, 1]
        P_NUM, P_DEN, P_TR = 2, 3, 4

        sm = lambda col, n=1: small[:, col:col + n]

        def fsz(ap):
            n = 1
            for d in list(ap.shape)[1:]:
                n *= int(d)
            return n

        def dma(eng, out_ap, in_ap, reads, writes):
            nb = fsz(out_ap) * int(list(out_ap.shape)[0]) * 2
            return S.op(eng, lambda e: e.dma_start(out=out_ap, in_=in_ap), reads=reads, writes=writes, dma=True, nbytes=nb)

        def act(out_ap, in_ap, func, reads, writes, bias=None, scale=None, accum=None):
            kw = {}
            if bias is not None:
                kw["bias"] = bias
            if scale is not None:
                kw["scale"] = scale
            if accum is not None:
                kw["accum_out"] = accum
            if func == AF.Copy and (bias is not None or (scale is not None and not isinstance(scale, float))):
                func = AF.Identity
            return S.op("act", lambda e: e.activation(out=out_ap, in_=in_ap, func=func, **kw), reads=reads, writes=writes,
                        cost=230 + 0.72 * fsz(out_ap))

        def barrier(names):
            S.op("dve", lambda e: e.memset(T1[:, 127:128], 0.0), reads=[], writes=[B(n) for n in names] + [B("bar")])

        def tt(eng, out_ap, a, b, op, reads, writes):
            return S.op(eng, lambda e: e.tensor_tensor(out=out_ap, in0=a, in1=b, op=op), reads=reads, writes=writes, cost=120 + 1.05 * fsz(out_ap))

        def ts(eng, out_ap, a, s1, s2, op0, op1, reads, writes):
            if op1 is None:
                return S.op(eng, lambda e: e.tensor_scalar(out=out_ap, in0=a, scalar1=s1, scalar2=None, op0=op0), reads=reads, writes=writes, cost=150 + 1.05 * fsz(out_ap))
            return S.op(eng, lambda e: e.tensor_scalar(out=out_ap, in0=a, scalar1=s1, scalar2=s2, op0=op0, op1=op1), reads=reads, writes=writes, cost=150 + 1.05 * fsz(out_ap))

        def stt(out_ap, a, s, b, op0, op1, reads, writes):
            return S.op("dve", lambda e: e.scalar_tensor_tensor(out=out_ap, in0=a, scalar=s, in1=b, op0=op0, op1=op1), reads=reads, writes=writes, cost=150 + 1.05 * fsz(out_ap))

        def cp(eng, out_ap, in_ap, reads, writes):
            if eng == "act":
                return S.op(eng, lambda e: e.activation(out=out_ap, in_=in_ap, func=AF.Copy), reads=reads, writes=writes, cost=230 + 0.72 * fsz(out_ap))
            return S.op(eng, lambda e: e.tensor_copy(out=out_ap, in_=in_ap), reads=reads, writes=writes, cost=(120 if eng == "dve" else 250) + 1.05 * fsz(out_ap))

        def mm(out_ap, lhsT, rhs, start, stop, reads, writes, skip=False):
            c_ = max(fsz(rhs), 64) / 2.4 * (4 if lhsT.dtype == F32 else 1) + 12
            return S.op("pe", lambda e: e.matmul(out_ap, lhsT=lhsT, rhs=rhs, start=start, stop=stop, skip_group_check=skip), reads=reads, writes=writes, cost=c_)

        def tr(out_ap, in_ap, ident, reads, writes):
            return S.op("pe", lambda e: e.transpose(out_ap, in_=in_ap, identity=ident), reads=reads, writes=writes, cost=(260 if in_ap.dtype == F32 else 110))

        cB = B("const")
        S.op("pool", lambda e: e.memset(ident_f[:], 1.0), writes=[cB])
        S.op("pool", lambda e: e.affine_select(out=ident_f[:], in_=ident_f[:], pattern=[[1, 128]], compare_op=ALU.is_equal, fill=0.0, base=0, channel_multiplier=-1), reads=[cB], writes=[cB])
        cp("pool", ident_b[:], ident_f[:], [cB], [cB])
        S.op("pool", lambda e: e.memset(ones_b[:], 1.0), writes=[cB])
        S.op("pool", lambda e: e.memset(ones_f[:], 1.0), writes=[cB])
        S.op("pool", lambda e: e.memset(cmask[:], 1.0), writes=[cB])
        for hs_ in (0, 64):
            S.op("pool", lambda e, hs_=hs_: e.affine_select(out=cmask[hs_:hs_ + 64, :], in_=cmask[hs_:hs_ + 64, :], pattern=[[1, 64]], compare_op=ALU.is_ge, fill=0.0, base=0, channel_multiplier=-1), reads=[cB], writes=[cB])
        S.op("pool", lambda e: e.memset(small[:], 0.0), writes=[cB])
        S.op("pool", lambda e: e.memset(RM[:], 1.0), writes=[cB])
        S.op("pool", lambda e: e.memset(RM[:].rearrange("p (c t) -> p c t", t=64)[:, :, 0:1], 0.0), reads=[cB], writes=[cB])
        S.op("pool", lambda e: e.memset(Sst[:], 0.0), writes=[B("Sst")])
        S.op("pool", lambda e: e.memset(sm(EPSC), EPS), reads=[cB], writes=[cB])
        dma("pool", BM[:].rearrange("p a b -> p (a b)"), bm_in[:, :], [], [B("BM")])
        dma("sp", gateB[:], bgate_row.partition_broadcast(128), [], [B("gateB")])
        dma("sp", fgB[:], fg_row.partition_broadcast(128), [], [B("fgB")])
        smB = B("small_in")
        dma("sp", sm(GS, 8), ng_fm[:, :], [cB], [smB])
        dma("sp", sm(SH, 16), bada_fm[:, :], [cB], [smB])
        dma("sp", sm(GN), gn_col[:, :], [cB], [smB])
        dma("sp", sm(FLAG), flag_in[:, :], [cB], [smB])
        dma("sp", sm(CS, 8), cfm[:, :], [cB], [smB])
        dma("sp", sm(TMPC, 16), lb_fm[:, :], [cB], [smB])
        act(sm(CS, 8), sm(CS, 8), AF.Silu, [smB, cB], [cB])
        cs2v = small[:, CS2:CS2 + 16].rearrange("p (a b) -> p a b", b=2)
        cp("dve", cs2v[:, :, 0], sm(CS, 8), [cB], [cB])
        cp("dve", cs2v[:, :, 1], sm(CS, 8), [cB], [cB])

        def ws_buf(kind, g):
            if kind == "in" and g in (6, 7, 8, 9):
                return B("ws_early")
            if kind == "in" and g in (1, 2):
                return B("ws_mid")
            return B("ws_late")

        for g in (6, 8, 7, 9, 1, 2):
            dma("pool", win_bf[g, :, :], w_in[:, g * 512:(g + 1) * 512], [], [ws_buf("in", g)])

        def emit_late_conversions():
            for g in (0, 3, 4, 5, 10, 11, 12, 13, 14, 15):
                dma("pool", win_bf[g, :, :], w_in[:, g * 512:(g + 1) * 512], [], [ws_buf("in", g)])
            for g in range(2):
                dma("pool", wa_bf[g, :, :], w_a[:, g * 512:(g + 1) * 512], [], [ws_buf("a", g)])
                dma("pool", wb_bf[g, :, :], w_b[:, g * 512:(g + 1) * 512], [], [ws_buf("b", g)])
                dma("pool", wo_bf[g, :, :], w_o[:, g * 512:(g + 1) * 512], [], [ws_buf("o", g)])

        wada_sb = [HQ, HF, HG, HE]
        wadaB = B("hg_tmp")

        def wada_view(kc_):
            t = wada_sb[kc_ // 2]
            return t[:].rearrange("p a b -> p (a b)")[:, (kc_ % 2) * 1024:(kc_ % 2 + 1) * 1024]

        modps = PS[0]
        for part in range(3):
            for k2 in range(8):
                dma("sp", wada_view(k2), w_ada[k2 * 128:(k2 + 1) * 128, part * 1024:(part + 1) * 1024], [], [B("hg_tmp%d" % k2)])
            if part < 2:
                for cc in range(8):
                    for k2 in range(8):
                        mm(modps[:, 2 * cc:2 * cc + 2], wada_view(k2)[:, cc * 128:(cc + 1) * 128], cs2v[:, k2, :],
                           k2 == 0, k2 == 7, [B("hg_tmp%d" % k2), cB], [B("PS0")])
                dst = small[:, SH + 8 * part: SH + 8 * part + 8]
                src = modps[:, 0:16].rearrange("p (a b) -> p a b", b=2)[:, :, 0]
                tt("dve", dst, dst, src, ALU.add, [smB, cB, B("PS0")], [cB])
            else:
                for hh in range(2):
                    for k2 in range(8):
                        S.op("dve", lambda e, k2=k2: e.tensor_copy(out=T1[:], in_=small[:, CS + k2:CS + k2 + 1].to_broadcast([128, 128])), reads=[cB, B("T1")], writes=[B("T1")])
                        mm(PS[1][:, :], T1[:], wada_view(k2)[:, hh * 512:(hh + 1) * 512], k2 == 0, k2 == 7, [B("hg_tmp%d" % k2), B("T1")], [B("PS1")])
                    tt("dve", gateB[:, hh * 512:(hh + 1) * 512], gateB[:, hh * 512:(hh + 1) * 512], PS[1][:, :], ALU.add, [B("gateB"), B("PS1")], [B("gateB")])
        ts("dve", sm(LBV, 8), sm(LBV, 8), 1.0, None, ALU.add, None, [cB], [cB])
        tt("dve", sm(GS, 8), sm(GS, 8), sm(LBV, 8), ALU.mult, [cB], [cB])
        tt("dve", sm(LBV, 8), sm(TMPC, 8), sm(TMPC + 8, 8), ALU.subtract, [cB], [cB])
        act(sm(LBV, 8), sm(LBV, 8), AF.Sigmoid, [cB], [cB])
        ts("dve", sm(OML, 8), sm(LBV, 8), -1.0, 1.0, ALU.mult, ALU.add, [cB], [cB])
        ts("dve", sm(TMPC), sm(FLAG), -1.0, -NEG, ALU.add, ALU.mult, [cB], [cB])
        for J in range(4):
            n = 128 - 32 * J
            cp("dve", small[0:n, NBIAS + J:NBIAS + J + 1], small[0:n, TMPC:TMPC + 1], [cB], [cB])
        if debug:
            dma("sp", dbg["dbg_mod"][:, :], small[:, 0:16], [cB], [B("dbg_mod")])

        wstate = {"n": 0}

        WSRC = {"in": win_bf, "a": wa_bf, "b": wb_bf, "o": wo_bf}

        def load_w(kind, g, nk=8):
            i = wstate["n"] % NWB
            wstate["n"] += 1
            wb = B("wbuf%d" % i)
            dma("sp", wbuf[i][:, 0:nk, :], WSRC[kind][g, :, :].rearrange("(k p) n -> p k n", p=128), [ws_buf(kind, g)], [wb])
            return wbuf[i], wb

        xstate = {"n": 0}

        def load_x(src, t0):
            i = xstate["n"] % 2
            xstate["n"] += 1
            dma("sp", XB[i][:], src[t0:t0 + 128, :], [], [B("XB%d" % i)])
            return XB[i], B("XB%d" % i)

        def norm_tile(src, t0):
            hB = B("hT")
            for s in range(4):
                xt, xb = load_x(src, t0 + 128 * s)
                XNt = (SC0t, SC1t)[s % 2]
                xnB = B(("XN", "XN1")[s % 2])
                ssb_ = B("ss%d" % s)
                act(XNt[:].bitcast(BF16)[:, 0:D], xt[:], AF.Square, [xb], [xnB, ssb_], accum=sm(SS + s))
                act(sm(RSTD + s), sm(SS + s), AF.Ln, [ssb_, cB], [ssb_], bias=sm(EPSC), scale=1.0 / D)
                act(sm(RSTD + s), sm(RSTD + s), AF.Exp, [ssb_], [ssb_], scale=-0.5)
                ts("dve", XNt[:], xt[:], sm(RSTD + s), None, ALU.mult, None, [xb, ssb_], [xnB])
                for half in range(2):
                    pb = PS[P_TR] if half == 0 else PS[P_ACC[1]]
                    pbB = B("PS%d" % (P_TR if half == 0 else P_ACC[1]))
                    for k4 in range(4):
                        k2 = half * 4 + k4
                        tr(pb[:, k4 * 128:(k4 + 1) * 128], XNt[:, k2 * 128:(k2 + 1) * 128], ident_f[:], [xnB, cB], [pbB])
                    for k4 in range(4):
                        k2 = half * 4 + k4
                        if k4 % 2 == 0:
                            act(hT[:, k2, s * 128:(s + 1) * 128], pb[:, k4 * 128:(k4 + 1) * 128], AF.Identity, [pbB, cB], [hB],
                                bias=sm(SH + k2), scale=sm(GS + k2))
                        else:
                            ts("dve", hT[:, k2, s * 128:(s + 1) * 128], pb[:, k4 * 128:(k4 + 1) * 128], sm(GS + k2), sm(SH + k2), ALU.mult, ALU.add,
                               [pbB, cB], [hB])

        accn = {"n": 0}

        def next_acc():
            i = P_ACC[accn["n"] % 2]
            accn["n"] += 1
            return PS[i], B("PS%d" % i)

        def proj_fm(wt, wb, ncol_chunks, evac):
            for cc in range(ncol_chunks):
                pt, pB_ = next_acc()
                for k2 in range(8):
                    mm(pt[:, :], wt[:, k2, cc * 128:(cc + 1) * 128], hT[:, k2, :], k2 == 0, k2 == 7, [wb, B("hT")], [pB_])
                evac(cc, pt, pB_)

        def proj_tm(wt, wb, evac):
            for s in range(4):
                pt, pB_ = next_acc()
                for k2 in range(8):
                    mm(pt[:, :], hT[:, k2, s * 128:(s + 1) * 128], wt[:, k2, :], k2 == 0, k2 == 7, [wb, B("hT")], [pB_])
                evac(s, pt, pB_)

        def flat(t):
            return t[:].rearrange("p a b -> p (a b)")

        def hgrn_group(g, own):
            hq, hf, hg_, he = B("HQ"), B("HF"), B("HG"), B("HE")
            t1B = B("T1")
            PSBh = PS[P_TR][:].bitcast(BF16)
            psbhB = B("PS%d" % P_TR)
            wt, wb = load_w("in", 6 + g)
            proj_fm(wt, wb, 4, lambda cc, pt, pB_: act(HF[:, cc, :], pt[:, :], AF.Sigmoid, [pB_], [hf]))
            wt, wb = load_w("in", 8 + g)
            if own:
                proj_tm(wt, wb, lambda s, pt, pB_: act(IT[:, s, :], pt[:, :], AF.Copy, [pB_], [B("IT")]))
            else:
                proj_tm(wt, wb, lambda s, pt, pB_: act(IT[:, s, :], pt[:, :], AF.Copy, [pB_, cB], [B("IT")], scale=sm(FLAG)))
            if own:
                wt, wb = load_w("in", 4 + g)
                proj_fm(wt, wb, 4, lambda cc, pt, pB_: act(HQ[:, cc, :], pt[:, :], AF.Silu, [pB_], [hq]))
            halfB = {}
            for nm in ("HF", "HG", "HE"):
                for h_ in range(2):
                    halfB[(nm, h_)] = B("%s_h%d" % (nm, h_))
            S.op("dve", lambda e: e.memset(T1[:, 126:127], 0.0), reads=[hq], writes=list(halfB.values()) + [hf, hg_, he, B("bar2")], cost=80)
            chains = []
            for h_ in range(2):
                hs = slice(2 * h_, 2 * h_ + 2)
                fB, gB, eB = halfB[("HF", h_)], halfB[("HG", h_)], halfB[("HE", h_)]
                HFh, HGh, HEh, HQh = HF[:, hs, :], HG[:, hs, :], HE[:, hs, :], HQ[:, hs, :]
                fl = lambda ap: ap.rearrange("p a b -> p (a b)")
                bvh = HEh.rearrange("p a (c t) -> p (a c) t", t=64)
                gvh = HGh.rearrange("p a (c t) -> p (a c) t", t=64)
                KD2h, KDh, QDh = KD2[:, hs, :], KD[:, hs, :], QD[:, hs, :]
                st = []
                for hh in (2 * h_, 2 * h_ + 1):
                    hd = 4 * g + hh
                    st.append(lambda hh=hh, hd=hd, fB=fB: ts("dve", HF[:, hh, :], HF[:, hh, :], sm(OML + hd), sm(LBV + hd), ALU.mult, ALU.add, [fB, cB], [fB]))
                st.append(lambda HFh=HFh, HGh=HGh, fB=fB, gB=gB: act(fl(HGh), fl(HFh), AF.Ln, [fB], [gB]))
                st.append(lambda HFh=HFh, fB=fB: ts("dve", fl(HFh), fl(HFh), -1.0, 1.0, ALU.mult, ALU.add, [fB], [fB]))
                for hh in (2 * h_, 2 * h_ + 1):
                    st.append(lambda hh=hh, gB=gB, eB=eB: S.op("dve", lambda e: e.tensor_tensor_scan(out=HE[:, hh, :], data0=RM[:, :], data1=HG[:, hh, :], initial=0.0, op0=ALU.mult, op1=ALU.add),
                                                               reads=[gB, cB], writes=[eB], cost=100 + 2.1 * 512))
                st.append(lambda bvh=bvh, gvh=gvh, gB=gB, eB=eB: tt("dve", gvh, bvh, bvh[:, :, 31:32].to_broadcast([128, 16, 64]), ALU.subtract, [eB], [gB]))
                st.append(lambda bvh=bvh, eB=eB, h_=h_: act(T1[:, 16 * h_:16 * h_ + 16], bvh[:, :, 63], AF.Exp, [eB], [t1B]))
                st.append(lambda bvh=bvh, eB=eB, h_=h_: act(T1[:, 64 + 16 * h_:64 + 16 * h_ + 16], bvh[:, :, 31], AF.Exp, [eB], [t1B]))
                st.append(lambda bvh=bvh, gvh=gvh, gB=gB, eB=eB: tt("dve", bvh, gvh, gvh[:, :, 63:64].to_broadcast([128, 16, 64]), ALU.subtract, [gB, t1B], [eB]))
                st.append(lambda HEh=HEh, eB=eB: act(fl(HEh), fl(HEh), AF.Exp, [eB], [eB], scale=-1.0))
                st.append(lambda KD2h=KD2h, HFh=HFh, HEh=HEh, fB=fB, eB=eB: tt("dve", fl(KD2h), fl(HFh), fl(HEh), ALU.mult, [fB, eB], [B("KD2")]))
                if own:
                    st.append(lambda HEh=HEh, HGh=HGh, gB=gB, eB=eB: act(fl(HEh), fl(HGh), AF.Exp, [gB, B("KD2")], [eB], scale=-1.0))
                    st.append(lambda KDh=KDh, HFh=HFh, HEh=HEh, fB=fB, eB=eB: tt("dve", fl(KDh), fl(HFh), fl(HEh), ALU.mult, [fB, eB], [B("KD")]))
                    st.append(lambda HEh=HEh, HGh=HGh, gB=gB, eB=eB: act(fl(HEh), fl(HGh), AF.Exp, [gB, B("KD")], [eB]))
                    st.append(lambda QDh=QDh, HQh=HQh, HEh=HEh, eB=eB: tt("dve", fl(QDh), fl(HQh), fl(HEh), ALU.mult, [hq, eB], [B("QD")]))
                chains.append(st)
            n0 = len(chains[0])
            for k_ in range(n0 + 2):
                if k_ < n0:
                    chains[0][k_]()
                if 0 <= k_ - 2 < n0:
                    chains[1][k_ - 2]()
            S.op("dve", lambda e: e.memset(T1[:, 125:126], 0.0), reads=list(halfB.values()), writes=[hf, hg_, he, B("bar3")], cost=80)
            if own:
                wt, wb = load_w("in", 10 + g)
                proj_fm(wt, wb, 4, lambda cc, pt, pB_: act(HQ[:, cc, :], pt[:, :], AF.Silu, [pB_, B("QD")], [hq]))
            chs = [(ch, slice(ch * 64, ch * 64 + 64), ch // 2, slice((ch % 2) * 64, (ch % 2) * 64 + 64)) for ch in range(8)]
            for fill in range(2):
                for (ch, sl, sub, psl) in chs[4 * fill:4 * fill + 4]:
                    col0 = (sub % 2) * 512
                    for hh in range(4):
                        tr(PSBh[psl, col0 + hh * 128:col0 + (hh + 1) * 128], KD2[:, hh, sl], ident_b[:], [B("KD2"), cB], [psbhB])
                cp("act", KT8[:, 2 * fill:2 * fill + 2, :].rearrange("p a b -> p (a b)"), PSBh[:, 0:1024], [psbhB], [B("KT8")])
            if own and not SKIP.get("s6"):
                for fill in range(2):
                    ab, abB = next_acc()
                    for (ch, sl, sub, psl) in chs[4 * fill:4 * fill + 4]:
                        for hh in range(4):
                            c0_ = (sub % 2) * 256 + hh * 64
                            mm(ab[psl, c0_:c0_ + 64], KD[:, hh, sl], QD[:, hh, sl], True, True, [B("KD"), B("QD")], [abB])
                    tt("dve", ATm8[:, 2 * fill:2 * fill + 2, :].rearrange("p a (h t) -> p (a h) t", t=64), ab[:, :].rearrange("p (a t) -> p a t", t=64),
                       cmask[:].unsqueeze(1).to_broadcast([128, 8, 64]), ALU.mult, [abB, cB], [B("ATm8")])
            for (ch, sl, sub, psl) in chs:
                ubi = P_ACC[ch % 2]
                ub, ubB = PS[ubi], B("PS%d" % ubi)
                for hh in range(4):
                    mm(ub[:, hh * 128:(hh + 1) * 128], KT8[psl, sub, hh * 128:(hh + 1) * 128], IT[psl, sub, hh * 128:(hh + 1) * 128], True, True,
                       [B("KT8"), B("IT")], [ubB])
                for hh in range(4):
                    hd = 4 * g + hh
                    idx = hh * 8 + ch
                    if own:
                        act(SB8[:, ch, hh, :], Sst[:, hd, :], AF.Copy, [B("Sst"), t1B, hf], [hf], scale=T1[:, 64 + idx:65 + idx])
                    stt(Sst[:, hd, :], Sst[:, hd, :], T1[:, idx:idx + 1], ub[:, hh * 128:(hh + 1) * 128], ALU.mult, ALU.add, [B("Sst"), ubB, t1B], [B("Sst")])
            if own and not SKIP.get("s7"):
                for hh in range(4):
                    pt, pB_ = next_acc()
                    for (ch, sl, sub, psl) in chs:
                        mm(pt[:, sl], SB8[:, ch, hh, :], QD[:, hh, sl], True, False, [hf, B("QD")], [pB_])
                        mm(pt[:, sl], IT[psl, sub, hh * 128:(hh + 1) * 128], ATm8[psl, sub, hh * 64:(hh + 1) * 64], False, True,
                           [B("IT"), B("ATm8")], [pB_])
                    cp("dve", OBt[:, hh, :], pt[:, :], [pB_], [hg_])
                    if SKIP.get("s7b"):
                        continue
                    tt("dve", OSQt[:, hh, :], OBt[:, hh, :], OBt[:, hh, :], ALU.mult, [hg_], [he])
                    if SKIP.get("s7e"):
                        continue
                    st_, sB_ = next_acc()
                    for hf_ in range(2):
                        mm(st_[:, hf_ * 256:(hf_ + 1) * 256], ones_f[:, :], OSQt[:, hh, hf_ * 256:(hf_ + 1) * 256], True, True, [he, cB], [sB_])
                    if SKIP.get("s7c"):
                        continue
                    act(OSQt[:, hh, :], st_[:, :], AF.Ln, [sB_, cB], [he], bias=sm(EPSC), scale=1.0 / 128)
                    act(OSQt[:, hh, :], OSQt[:, hh, :], AF.Exp, [he], [he], scale=-0.5)
                    if SKIP.get("s7d"):
                        continue
                    stt(OBt[:, hh, :], OBt[:, hh, :], sm(GN), OSQt[:, hh, :], ALU.mult, ALU.mult, [hg_, he, cB], [hg_])
                    tt("dve", obT[:, 4 * g + hh, :], OBt[:, hh, :], HQ[:, hh, :], ALU.mult, [hg_, hq], [B("obT")])

        def attention(J):
            nB, dB, ssB, psbB = B("PS%d" % P_NUM), B("PS%d" % P_DEN), B("PSS"), B("PSB")
            kcB, vcB, qB = B("kc"), B("vc"), B("qT")
            for c in range(4):
                first = {0: True, 64: True}
                batches = []
                for p in range(3):
                    def build_v(p=p):
                        if p == 0:
                            ktiles = [(2048 + 128 * n, 1, 128) for n in (-1, 3, 0, 1, 2)]
                        elif p == 1:
                            ktiles = []
                            for r in range(4):
                                ktiles += [(2048 + r, 4, 128), (1536 + r, 4, 128)]
                        else:
                            ktiles = [(r, 16, 128) for r in range(16)]
                        for rnd in range(0, len(ktiles), 8):
                            grp = ktiles[rnd:rnd + 8]
                            for i_, (c0, stp, nk) in enumerate(grp):
                                tr(PSB[:, i_ * 128:(i_ + 1) * 128], vc[:, c, c0:c0 + stp * (nk - 1) + 1:stp], ident_b[:], [vcB, cB], [psbB])
                            cp("act", VA[p % 2][:, rnd * 128:(rnd + len(grp)) * 128], PSB[:, 0:len(grp) * 128], [psbB], [B("VA%d" % (p % 2))])
                        if p == 2:
                            for rnd in range(0, 16, 8):
                                for i_ in range(8):
                                    r = rnd + i_
                                    tr(PSB[0:32, i_ * 128:(i_ + 1) * 128], vc[:, c, 2048 + r:2048 + r + 16 * 31 + 1:16], ident_b[:], [vcB, cB], [psbB])
                                cp("act", VB[:, rnd * 128:(rnd + 8) * 128], PSB[0:32, 0:1024], [psbB], [B("VB")])
                    for hs in (0, 64):
                        batches.append((p, hs, build_v if hs == 0 else None))

                def qk_and_softmax(bi, p, hs, build_v):
                    hsl = slice(hs, hs + 64)
                    h = 2 * c + hs // 64
                    bmh = BM[:, p * 8 + h, :]
                    sc, scB = SC[bi % 2], B("SC%d" % (bi % 2))
                    pt, ptB = PT[bi % 2], B("PT%d" % (bi % 2))
                    if p == 0:
                        blocks = [(-1, 0, 128), (3, 128, 128), (0, 256, 256), (1, 512, 256), (2, 768, 256)]
                        for (n, col, w) in blocks:
                            q0 = 0 if n == -1 else 128 * n
                            kc0 = 2048 + 128 * n
                            mm(PSS[:, col:col + w], kc[hsl, c, kc0:kc0 + 128], qT[hsl, c, q0:q0 + w], True, True, [kcB, qB], [ssB])
                        tt("dve", sc[:, 0:128], PSS[:, 0:128], bmh[:, 128:256], ALU.add, [ssB, B("BM")], [scB])
                        tt("dve", sc[:, 128:256], PSS[:, 128:256], bmh[:, 0:128], ALU.add, [ssB, B("BM")], [scB])
                        tt("dve", sc[:, 256:1024].rearrange("p (a b) -> p a b", b=256), PSS[:, 256:1024].rearrange("p (a b) -> p a b", b=256),
                           bmh.unsqueeze(1).to_broadcast([128, 3, 256]), ALU.add, [ssB, B("BM")], [scB])
                        if J == 0:
                            act(pt[:, 0:128], sc[:, 0:128], AF.Exp, [scB, cB], [ptB], bias=sm(NBIAS))
                            act(pt[:, 128:1024], sc[:, 128:1024], AF.Exp, [scB], [ptB])
                        else:
                            act(pt[:, :], sc[:, :], AF.Exp, [scB], [ptB])
                    elif p == 1:
                        for r in range(4):
                            qv = qT[hsl, c, r:r + 4 * 127 + 1:4]
                            mm(PSS[:, r * 256:r * 256 + 128], kc[hsl, c, 2048 + r:2048 + r + 4 * 127 + 1:4], qv, True, True, [kcB, qB], [ssB])
                            mm(PSS[:, r * 256 + 128:r * 256 + 256], kc[hsl, c, 1536 + r:1536 + r + 4 * 127 + 1:4], qv, True, True, [kcB, qB], [ssB])
                        tt("dve", sc[:, :].rearrange("p (a b) -> p a b", b=256), PSS[:, :].rearrange("p (a b) -> p a b", b=256),
                           bmh.unsqueeze(1).to_broadcast([128, 4, 256]), ALU.add, [ssB, B("BM")], [scB])
                        if J == 0:
                            scv = sc[:, :].rearrange("p (a t b) -> p a t b", t=2, b=128)
                            ptv = pt[:, :].rearrange("p (a t b) -> p a t b", t=2, b=128)
                            act(ptv[:, :, 0, :], scv[:, :, 0, :], AF.Exp, [scB], [ptB])
                            act(ptv[:, :, 1, :], scv[:, :, 1, :], AF.Exp, [scB, cB], [ptB], bias=sm(NBIAS))
                        else:
                            act(pt[:, :], sc[:, :], AF.Exp, [scB], [ptB])
                    else:
                        for r in range(16):
                            qv = qT[hsl, c, r:r + 16 * 31 + 1:16]
                            mm(PSS[:, r * 32:(r + 1) * 32], kc[hsl, c, r:r + 16 * 127 + 1:16], qv, True, True, [kcB, qB], [ssB])
                        for r in range(16):
                            qv = qT[hsl, c, r:r + 16 * 31 + 1:16]
                            mm(PSS[0:32, 512 + r * 32:512 + (r + 1) * 32], kc[hsl, c, 2048 + r:2048 + r + 16 * 31 + 1:16], qv, True, True, [kcB, qB], [ssB])
                        tt("dve", sc[:, 0:512].rearrange("p (a b) -> p a b", b=32), PSS[:, 0:512].rearrange("p (a b) -> p a b", b=32),
                           bmh[:, 128:160].unsqueeze(1).to_broadcast([128, 16, 32]), ALU.add, [ssB, B("BM")], [scB])
                        tt("dve", sc[0:32, 512:1024].rearrange("p (a b) -> p a b", b=32), PSS[0:32, 512:1024].rearrange("p (a b) -> p a b", b=32),
                           bmh[0:32, 0:32].unsqueeze(1).to_broadcast([32, 16, 32]), ALU.add, [ssB, B("BM")], [scB])
                        if J < 4:
                            act(pt[:, 0:512], sc[:, 0:512], AF.Exp, [scB, cB], [ptB], bias=sm(NBIAS + J))
                        else:
                            act(pt[:, 0:512], sc[:, 0:512], AF.Exp, [scB], [ptB])
                        act(pt[0:32, 512:1024], sc[0:32, 512:1024], AF.Exp, [scB], [ptB])

                def pv(bi, p, hs, last):
                    hsl = slice(hs, hs + 64)
                    pt, ptB = PT[bi % 2], B("PT%d" % (bi % 2))
                    items = []
                    if p == 0:
                        blocks = [(-1, 0, 128), (3, 128, 128), (0, 256, 256), (1, 512, 256), (2, 768, 256)]
                        for i_, (n, col, w) in enumerate(blocks):
                            q0 = 0 if n == -1 else 128 * n
                            items.append((VA[0][:, i_ * 128 + hs:i_ * 128 + hs + 64], ones_b[:, 0:64], pt[:, col:col + w], slice(q0, q0 + w), [B("VA0")]))
                    elif p == 1:
                        for r in range(4):
                            for t_ in range(2):
                                i_ = 2 * r + t_
                                items.append((VA[1][:, i_ * 128 + hs:i_ * 128 + hs + 64], ones_b[:, 0:64], pt[:, i_ * 128:(i_ + 1) * 128],
                                              slice(r, r + 4 * 127 + 1, 4), [B("VA1")]))
                    else:
                        for r in range(16):
                            items.append((VA[0][:, r * 128 + hs:r * 128 + hs + 64], ones_b[:, 0:64], pt[:, r * 32:(r + 1) * 32],
                                          slice(r, r + 16 * 31 + 1, 16), [B("VA0")]))
                        for r in range(16):
                            items.append((VB[:, r * 128 + hs:r * 128 + hs + 64], ones_b[0:32, 0:64], pt[0:32, 512 + r * 32:512 + (r + 1) * 32],
                                          slice(r, r + 16 * 31 + 1, 16), [B("VB")]))
                    for k_, (vap, oap, rhs, qsl, vb) in enumerate(items):
                        st_ = first[hs]
                        first[hs] = False
                        sp_ = last and (k_ == len(items) - 1)
                        mm(PS[P_NUM][hsl, qsl], vap, rhs, st_, sp_, vb + [ptB], [nB], skip=True)
                        mm(PS[P_DEN][hsl, qsl], oap, rhs, st_, sp_, [cB, ptB], [dB], skip=True)

                nb_ = len(batches)
                for bi in range(nb_ + 1):
                    if bi < nb_:
                        p, hs, bv_ = batches[bi]
                        if bv_ is not None:
                            bv_()
                        qk_and_softmax(bi, p, hs, bv_)
                    if bi >= 1:
                        p, hs, _ = batches[bi - 1]
                        pv(bi - 1, p, hs, last=(p == 2))
                S.op("dve", lambda e: e.reciprocal(out=RD, in_=PS[P_DEN][:, :]), reads=[dB], writes=[B("SC0")], cost=650)
                tt("dve", RD, PS[P_NUM][:, :], RD, ALU.mult, [nB, B("SC0")], [B("SC0")])
                tt("dve", oaT[:, c, :], RD, sz[:, c, :], ALU.mult, [B("SC0"), B("sz")], [B("oaT")])

        def merge_and_out(J):
            hq, hf = B("HQ"), B("HF")
            for half in range(2):
                wt, wb = load_w("in", 12 + half)
                proj_fm(wt, wb, 4, lambda cc, pt, pB_: act(HQ[:, cc, :], pt[:, :], AF.Sigmoid, [pB_], [hq]))
                wt, wb = load_w("in", 14 + half)
                proj_fm(wt, wb, 4, lambda cc, pt, pB_: act(HF[:, cc, :], pt[:, :], AF.Sigmoid, [pB_], [hf]))
                wta, wba = load_w("a", half, 4)
                for cc in range(4):
                    pt, pB_ = next_acc()
                    for k2 in range(4):
                        mm(pt[:, :], wta[:, k2, cc * 128:(cc + 1) * 128], oaT[:, k2, :], k2 == 0, k2 == 3, [wba, B("oaT")], [pB_])
                    tt("dve", HQ[:, cc, :], HQ[:, cc, :], pt[:, :], ALU.mult, [hq, pB_], [hq])
                wtb, wbb = load_w("b", half)
                for cc in range(4):
                    pt, pB_ = next_acc()
                    for k2 in range(8):
                        mm(pt[:, :], wtb[:, k2, cc * 128:(cc + 1) * 128], obT[:, k2, :], k2 == 0, k2 == 7, [wbb, B("obT")], [pB_])
                    tt("dve", HF[:, cc, :], HF[:, cc, :], pt[:, :], ALU.mult, [hf, pB_], [hf])
                    tt("dve", yT[:, 4 * half + cc, :], HF[:, cc, :], HQ[:, cc, :], ALU.add, [hf, hq], [B("yT")])
            wts = [load_w("o", n) for n in range(2)]
            for s in range(4):
                xt, xb = load_x(xo, J * TT + 128 * s)
                XNt = (SC0t, SC1t)[s % 2]
                xnB = B(("XN", "XN1")[s % 2])
                ssb_ = B("ss%d" % s)
                for n in range(2):
                    wt, wb = wts[n]
                    pt, pB_ = next_acc()
                    for k2 in range(8):
                        mm(pt[:, :], yT[:, k2, s * 128:(s + 1) * 128], wt[:, k2, :], k2 == 0, k2 == 7, [wb, B("yT")], [pB_])
                    tt("dve", XNt[:, n * 512:(n + 1) * 512], pt[:, :], gateB[:, n * 512:(n + 1) * 512], ALU.mult, [pB_, B("gateB")], [xnB])
                tt("dve", XNt[:], XNt[:], xt[:], ALU.add, [xnB, xb], [xnB])
                act(PT[s % 2], XNt[:], AF.Square, [xnB], [B("PT%d" % (s % 2)), ssb_], accum=sm(SS + s))
                act(sm(RSTD + s), sm(SS + s), AF.Ln, [ssb_, cB], [ssb_], bias=sm(EPSC), scale=1.0 / D)
                act(sm(RSTD + s), sm(RSTD + s), AF.Exp, [ssb_], [ssb_], scale=-0.5)
                stt(xt[:], XNt[:], sm(RSTD + s), fgB[:], ALU.mult, ALU.mult, [xnB, ssb_, B("fgB"), xb], [xb])
                r = dma("sp", out[J * TT + 128 * s:J * TT + 128 * (s + 1), :], xt[:], [xb], [B("out")])
                out_dmas.append(r)

        out_dmas = []

        barrier(["hg_tmp%d" % i for i in range(8)] + ["HQ", "HF", "HG", "HE", "SC0", "SC1", "PT0", "PT1", "RD", "VA0", "VA1", "sz", "QD", "KD", "yT", "XN", "XN1"])
        first_pre = NT - n_pre_tiles
        for j in range(first_pre, NT):
            norm_tile(xp, j * TT)
            if j >= 4:
                pos = (j - 4) * TT
                wt, wb = load_w("in", 1)
                proj_fm(wt, wb, 4, lambda cc, pt, pB_, pos=pos: cp("act", kc[:, cc, pos:pos + TT], pt[:, :], [pB_], [B("kc")]))
                wt, wb = load_w("in", 2)
                proj_fm(wt, wb, 4, lambda cc, pt, pB_, pos=pos: cp("act", vc[:, cc, pos:pos + TT], pt[:, :], [pB_], [B("vc")]))
            for g in range(2):
                hgrn_group(g, own=False)
            if j == first_pre:
                emit_late_conversions()
        if n_pre_tiles == 0:
            emit_late_conversions()
        if debug:
            dma("sp", dbg["dbg_S"][:, :], Sst[:].rearrange("p a b -> p (a b)"), [B("Sst")], [B("dbg_S")])
        ALLB = ["HQ", "HF", "HG", "HE", "SC0", "SC1", "PT0", "PT1", "VA0", "VA1", "sz", "QD", "KD", "KD2", "yT", "XN", "XN1"] + ["hg_tmp%d" % i for i in range(8)]
        for J in range(n_own_tiles):
            norm_tile(xo, J * TT)
            if debug and J == 0:
                dma("sp", dbg["dbg_hT"][:, :], hT[:].rearrange("p a b -> p (a b)"), [B("hT")], [B("dbg_hT")])
            barrier(ALLB)
            wt, wb = load_w("in", 0)
            proj_fm(wt, wb, 4, lambda cc, pt, pB_: act(qT[:, cc, :], pt[:, :], AF.Copy, [pB_], [B("qT")], scale=0.125))
            wt, wb = load_w("in", 1)
            proj_fm(wt, wb, 4, lambda cc, pt, pB_: cp("act", kc[:, cc, 2048:2560], pt[:, :], [pB_], [B("kc")]))
            wt, wb = load_w("in", 2)
            proj_fm(wt, wb, 4, lambda cc, pt, pB_: cp("act", vc[:, cc, 2048:2560], pt[:, :], [pB_], [B("vc")]))
            wt, wb = load_w("in", 3)
            proj_fm(wt, wb, 4, lambda cc, pt, pB_: act(sz[:, cc, :], pt[:, :], AF.Silu, [pB_], [B("sz")]))
            if debug and J == 0:
                dma("sp", dbg["dbg_q"][:, :], qT[:].rearrange("p a b -> p (a b)"), [B("qT")], [B("dbg_q")])
            S.begin_capture()
            attention(J)
            if J < n_own_tiles - 1:
                for blk in range(4):
                    cp("pool", kc[:, :, blk * 512:(blk + 1) * 512], kc[:, :, (blk + 1) * 512:(blk + 2) * 512], [B("kc")], [B("kc")])
                    cp("pool", vc[:, :, blk * 512:(blk + 1) * 512], vc[:, :, (blk + 1) * 512:(blk + 2) * 512], [B("vc")], [B("vc")])
            stA = S.end_capture()
            S.begin_capture()
            for g in range(2):
                hgrn_group(g, own=True)
            stB = S.end_capture()
            if MERGE:
                S.merge([stA, stB])
            else:
                S.merge([stA])
                S.merge([stB])
            barrier(ALLB)
            if debug and J == 0:
                dma("sp", dbg["dbg_oa"][:, :], oaT[:].rearrange("p a b -> p (a b)"), [B("oaT")], [B("dbg_oa")])
                dma("sp", dbg["dbg_ob"][:, :], obT[:].rearrange("p a b -> p (a b)"), [B("obT")], [B("dbg_ob")])
            merge_and_out(J)
            barrier(ALLB)
            if debug and J == 0:
                dma("sp", dbg["dbg_y"][:, :], yT[:].rearrange("p a b -> p (a b)"), [B("yT")], [B("dbg_y")])

        S.finalize()
        finals = []
        seen = set()
        for b_ in bufs.values():
            if b_.dsem is not None and (b_.name == "out" or b_.name.startswith("dbg")):
                finals.append((b_.dsem, b_.dcnt))
        for b_ in bufs.values():
            if b_.dsem is not None and not (b_.name == "out" or b_.name.startswith("dbg")):
                finals.append((b_.dsem, b_.dcnt))
        with nc.Block() as block:
            @block.tensor
            def _(e):
                S.emit_engine("pe", e)

            @block.scalar
            def _(e):
                S.emit_engine("act", e)

            @block.vector
            def _(e):
                S.emit_engine("dve", e)

            @block.gpsimd
            def _(e):
                S.emit_engine("pool", e)

            @block.sync
            def _(e):
                S.emit_engine("sp", e, final_waits=finals)
    return nc, len(S.ins)


def _t5_bucket_np(dist):
    n_buckets, max_distance = 32, 2048
    max_exact = n_buckets // 2
    n = dist.astype(np.float32)
    large = max_exact + (np.log(np.maximum(n, np.float32(1.0)) / np.float32(max_exact))
                         / np.float32(math.log(max_distance / max_exact))
                         * np.float32(n_buckets - max_exact)).astype(np.int32)
    large = np.minimum(large, n_buckets - 1)
    return np.where(dist < max_exact, dist, large)


def _bias_tiles(rel_bias):
    kl = np.arange(128)[:, None]
    qq = np.arange(256)[None, :]
    delta = qq - kl
    valid = (delta >= 0) & (delta <= 128)
    outp = np.full((128, 3, 8, 256), NEG, dtype=np.float32)
    for p, d in enumerate((1, 4, 16)):
        bucket = _t5_bucket_np(np.clip(delta, 0, None) * d)
        g = rel_bias[bucket]
        for h in range(8):
            outp[:, p, h, :] = np.where(valid, g[:, :, h], np.float32(NEG))
    return np.ascontiguousarray(outp.reshape(128, 24 * 256))


def _fm(v, nchunk):
    return np.ascontiguousarray(np.asarray(v, dtype=np.float32).reshape(nchunk, 128).T)


_PROGRAM = {}


def kernel(x, c, w_ada, b_ada, norm_g, w_in, hgrn_onorm_g, w_branch_a, w_branch_b,
           w_out, rel_bias, hgrn_lb, final_g, _debug=False, _tiles=None):
    x = np.asarray(x, dtype=np.float32)
    key = (bool(_debug), _tiles)
    if key not in _PROGRAM:
        if _tiles is None:
            _PROGRAM[key] = build_program(debug=_debug)
        else:
            _PROGRAM[key] = build_program(n_own_tiles=_tiles[0], n_pre_tiles=_tiles[1], debug=_debug)
    nc, _ = _PROGRAM[key]
    b_ada0 = np.asarray(b_ada, dtype=np.float32)[0]
    shared = {
        "w_ada": np.ascontiguousarray(np.asarray(w_ada, dtype=np.float32)[0]),
        "bada_fm": _fm(b_ada0[:2048], 16),
        "bgate_row": np.ascontiguousarray(b_ada0[2048:].reshape(1, D)),
        "ng_fm": _fm(np.asarray(norm_g)[0], 8),
        "w_in": np.ascontiguousarray(np.asarray(w_in, dtype=np.float32)[0]),
        "gn_col": np.ascontiguousarray(np.asarray(hgrn_onorm_g, dtype=np.float32)[0].reshape(128, 1)),
        "w_a": np.ascontiguousarray(np.asarray(w_branch_a, dtype=np.float32)[0]),
        "w_b": np.ascontiguousarray(np.asarray(w_branch_b, dtype=np.float32)[0]),
        "w_o": np.ascontiguousarray(np.asarray(w_out, dtype=np.float32)[0]),
        "bm": _bias_tiles(np.asarray(rel_bias, dtype=np.float32)),
        "lb_fm": np.ascontiguousarray(np.concatenate([_fm(np.asarray(hgrn_lb)[0], 8), _fm(np.asarray(hgrn_lb)[1], 8)], axis=1)),
        "fg_row": np.ascontiguousarray(np.asarray(final_g, dtype=np.float32).reshape(1, D)),
    }
    in_maps = []
    for core in range(8):
        b, half = core // 2, core % 2
        m = dict(shared)
        m["xo"] = np.ascontiguousarray(x[b, half * TOWN:(half + 1) * TOWN])
        m["xp"] = np.ascontiguousarray(x[b, 0:TOWN]) if half == 1 else np.zeros((TOWN, D), np.float32)
        m["cfm"] = _fm(np.asarray(c, dtype=np.float32)[b], 8)
        m["flag"] = np.full((128, 1), float(half), np.float32)
        in_maps.append(m)
    res = run_bass_kernel_spmd(nc, in_maps, core_ids=list(range(8)))
    outp = np.empty((NB, SEQ, D), np.float32)
    for core in range(8):
        b, half = core // 2, core % 2
        outp[b, half * TOWN:(half + 1) * TOWN] = res.results[core]["out"]
    if _debug:
        return outp, res.results
    return outp
```
